# Optimizing a Trainium2 kernel written in Bass

```python
import math
import jax
import jax.numpy as jnp
from jax import lax
import numpy as np

D_MODEL = 1024
BATCH = 16
SEQ = 4096
DEPTH = 4

GRID_W = 64
CTX_LEN = 256
N_MIXERS = 4
D_FF = 4 * D_MODEL
N_MOD = 6
EPS = 1e-6
CONV_W = 3
HG_HEAD_DIM = 128
HG_HEADS = D_MODEL // HG_HEAD_DIM
HG_CHUNK = 32
S5_GROUP = 16
S5_GROUPS = D_MODEL // S5_GROUP
S5_STATE = 64
S5_DT_MIN = 1e-3
S5_DT_MAX = 1e-1
NA_HEADS = 16
NA_HEAD_DIM = D_MODEL // NA_HEADS
NA_ROWS = 8
NA_COLS = 16
NA_STRIP = 2 * NA_COLS

kernel_name = 'hybrid_interleaved_dit_block'


def _layers_of(mixer):
    return len(range(mixer, DEPTH, N_MIXERS))


def rmsnorm(x, g):
    xf = x.astype(jnp.float32)
    y = xf * lax.rsqrt(jnp.mean(xf * xf, axis=-1, keepdims=True) + EPS)
    return (y * g.astype(jnp.float32)).astype(x.dtype)


def modulate(h, shift, scale):
    return h * (1 + scale) + shift


def squared_relu_mlp(h, w_in, w_out):
    return jnp.square(jax.nn.relu(h @ w_in)) @ w_out


def depthwise_conv3(u, w):
    return lax.conv_general_dilated(u, w.astype(u.dtype)[:, None, :], window_strides=(1,),
                                    padding=((1, 1),), dimension_numbers=('NWC', 'WIO', 'NWC'),
                                    feature_group_count=u.shape[-1])


def short_gated_conv(h, w_in, conv_w, w_out):
    b_gate, c_gate, v = jnp.split(h @ w_in, 3, axis=-1)
    return (b_gate * depthwise_conv3(c_gate * v, conv_w)) @ w_out


def gla_chunked(q, k, v, log_f, s0):
    bsz, seq, nh, dk = q.shape
    dv = v.shape[-1]
    nc = seq // HG_CHUNK

    def chunks(t):
        return t.reshape(bsz, nc, HG_CHUNK, nh, t.shape[-1]).transpose(1, 0, 3, 2, 4)

    qc, kc, vc = chunks(q.astype(jnp.float32)), chunks(k.astype(jnp.float32)), chunks(v)
    b = jnp.cumsum(chunks(log_f.astype(jnp.float32)), axis=3)
    b_mid = b[:, :, :, HG_CHUNK // 2 - 1:HG_CHUNK // 2]
    b_last = b[:, :, :, -1:]
    scores = jnp.einsum('nbhtk,nbhsk->nbhts', qc * jnp.exp(b - b_mid), kc * jnp.exp(b_mid - b))
    prefix = jnp.tril(jnp.ones((HG_CHUNK, HG_CHUNK), dtype=bool))
    o_intra = jnp.einsum('nbhts,nbhsv->nbhtv', jnp.where(prefix, scores, 0.0), vc)
    q_out = qc * jnp.exp(b)
    k_state = kc * jnp.exp(b_last - b)
    decay_chunk = jnp.exp(b_last[:, :, :, 0, :])

    def step(s, xs):
        q_o, k_s, v_s, a = xs
        o = jnp.einsum('bhtk,bhkv->bhtv', q_o, s)
        s = a[..., None] * s + jnp.einsum('bhtk,bhtv->bhkv', k_s, v_s)
        return s, o

    s_fin, o_inter = lax.scan(step, s0, (q_out, k_state, vc, decay_chunk))
    o = (o_intra + o_inter).transpose(1, 0, 3, 2, 4).reshape(bsz, seq, nh, dv)
    return o, s_fin


def hgrn2_mixer(h_ctx, h_lat, w_in, lower_bound, g_norm, w_out, ctx_out):
    bsz = h_lat.shape[0]
    s0 = jnp.zeros((bsz, HG_HEADS, HG_HEAD_DIM, HG_HEAD_DIM), jnp.float32)

    def heads(t):
        return t.reshape(*t.shape[:2], HG_HEADS, HG_HEAD_DIM)

    def forget(t):
        f = lower_bound + (1 - lower_bound) * jax.nn.sigmoid(t.astype(jnp.float32))
        return heads(1 - f), heads(jnp.log(f))

    def project(h):
        q, inp, gate, f_fwd, f_bwd = jnp.split(h @ w_in, 5, axis=-1)
        return heads(q), heads(inp), gate, forget(f_fwd), forget(f_bwd)

    def flip(t):
        return t[:, ::-1]

    qc, vc, gc, (kcf, lcf), (kcb, lcb) = project(h_ctx)
    ql, vl, gl, (klf, llf), (klb, llb) = project(h_lat)
    oc_f, st_f = gla_chunked(qc, kcf, vc, lcf, s0)
    ol_f, _ = gla_chunked(ql, klf, vl, llf, st_f)
    oc_b, st_b = gla_chunked(flip(qc), flip(kcb), flip(vc), flip(lcb), s0)
    ol_b, _ = gla_chunked(flip(ql), flip(klb), flip(vl), flip(llb), st_b)

    def readout(o, gate):
        o = rmsnorm(o, g_norm.reshape(HG_HEADS, HG_HEAD_DIM))
        o = o.reshape(*o.shape[:2], D_MODEL).astype(gate.dtype)
        return (o * jax.nn.silu(gate)) @ w_out

    y_lat = readout(ol_f + flip(ol_b), gl)
    y_ctx = readout(oc_f + flip(oc_b), gc) if ctx_out else None
    return y_ctx, y_lat


def s5_discretise(lam_re, lam_im, log_dt, b_re, b_im):
    lam_re = jnp.minimum(lam_re.astype(jnp.float32), -1e-4)
    lam_im = lam_im.astype(jnp.float32)
    dt = jnp.exp(log_dt.astype(jnp.float32))[:, None]
    mag = jnp.exp(lam_re * dt)
    a_re, a_im = mag * jnp.cos(lam_im * dt), mag * jnp.sin(lam_im * dt)
    den = lam_re * lam_re + lam_im * lam_im
    f_re = ((a_re - 1) * lam_re + a_im * lam_im) / den
    f_im = (a_im * lam_re - (a_re - 1) * lam_im) / den
    b_re, b_im = b_re.astype(jnp.float32), b_im.astype(jnp.float32)
    bb_re = f_re[..., None] * b_re - f_im[..., None] * b_im
    bb_im = f_re[..., None] * b_im + f_im[..., None] * b_re
    return a_re, a_im, bb_re, bb_im


def s5_scan(bu_re, bu_im, a_re, a_im, init, reverse):
    if init is not None:
        i_re, i_im = init
        first = -1 if reverse else 0
        bu_re = bu_re.at[:, first].add(a_re * i_re - a_im * i_im)
        bu_im = bu_im.at[:, first].add(a_re * i_im + a_im * i_re)
    seq = bu_re.shape[1]
    a_seq_re = jnp.broadcast_to(a_re, (1, seq) + a_re.shape)
    a_seq_im = jnp.broadcast_to(a_im, (1, seq) + a_im.shape)

    def combine(e1, e2):
        a1r, a1i, b1r, b1i = e1
        a2r, a2i, b2r, b2i = e2
        return (a2r * a1r - a2i * a1i, a2r * a1i + a2i * a1r,
                a2r * b1r - a2i * b1i + b2r, a2r * b1i + a2i * b1r + b2i)

    _, _, x_re, x_im = lax.associative_scan(combine, (a_seq_re, a_seq_im, bu_re, bu_im),
                                            axis=1, reverse=reverse)
    return x_re, x_im


def s5_mixer(h_ctx, h_lat, lam_re, lam_im, log_dt, b_re, b_im, c_re, c_im, d_skip, w_glu, ctx_out):
    def grouped(h):
        return h.reshape(*h.shape[:2], S5_GROUPS, S5_GROUP).astype(jnp.float32)

    u_c, u_l = grouped(h_ctx), grouped(h_lat)
    d = d_skip.reshape(S5_GROUPS, S5_GROUP).astype(jnp.float32)
    y_c, y_l = d * u_c, d * u_l
    for direction in range(2):
        reverse = direction == 1
        a_re, a_im, bb_re, bb_im = s5_discretise(lam_re[direction], lam_im[direction],
                                                 log_dt[direction], b_re, b_im)
        cr, ci = c_re[direction].astype(jnp.float32), c_im[direction].astype(jnp.float32)

        def drive(u):
            return (jnp.einsum('bsgc,gnc->bsgn', u, bb_re), jnp.einsum('bsgc,gnc->bsgn', u, bb_im))

        def readout(xr, xi):
            return jnp.einsum('bsgn,gcn->bsgc', xr, cr) - jnp.einsum('bsgn,gcn->bsgc', xi, ci)

        xc_re, xc_im = s5_scan(*drive(u_c), a_re, a_im, None, reverse)
        end = 0 if reverse else -1
        xl_re, xl_im = s5_scan(*drive(u_l), a_re, a_im, (xc_re[:, end], xc_im[:, end]), reverse)
        y_l = y_l + readout(xl_re, xl_im)
        if ctx_out:
            y_c = y_c + readout(xc_re, xc_im)

    def glu(y):
        z = jax.nn.gelu(y.reshape(*y.shape[:2], D_MODEL)).astype(w_glu.dtype)
        val, gate = jnp.split(z @ w_glu, 2, axis=-1)
        return val * jax.nn.sigmoid(gate)

    return (glu(y_c) if ctx_out else None), glu(y_l)


def neighbourhood_attention(h_ctx, h_lat, w_qkv, rpb, w_out, ctx_out):
    bsz, seq, _ = h_lat.shape
    rows = seq // GRID_W
    kr = min(NA_ROWS, rows)
    n_cb = GRID_W // NA_COLS
    n_loc = kr * NA_STRIP
    scale = NA_HEAD_DIM ** -0.5

    def heads(t):
        return t.reshape(*t.shape[:2], NA_HEADS, NA_HEAD_DIM)

    q_l, k_l, v_l = (heads(t) for t in jnp.split(h_lat @ w_qkv, 3, axis=-1))
    k_c, v_c = (heads(t) for t in jnp.split(h_ctx @ w_qkv[:, D_MODEL:], 2, axis=-1))

    r = jnp.arange(rows)
    key_rows = jnp.clip(r - kr // 2, 0, rows - kr)[:, None] + jnp.arange(kr)
    qcol = (jnp.arange(n_cb) * NA_COLS)[:, None] + jnp.arange(NA_COLS)
    key_cols = (jnp.clip(jnp.arange(n_cb) * NA_COLS - NA_COLS // 2, 0, GRID_W - NA_STRIP)[:, None]
                + jnp.arange(NA_STRIP))
    q_start = jnp.clip(qcol - NA_COLS // 2, 0, GRID_W - NA_COLS)[..., None]
    kcol = key_cols[:, None, :]
    in_win = jnp.tile((kcol >= q_start) & (kcol < q_start + NA_COLS), (1, 1, kr))
    dr = key_rows - r[:, None] + (NA_ROWS - 1)
    dc = jnp.clip(kcol - qcol[..., None], 1 - NA_COLS, NA_COLS - 1) + (NA_COLS - 1)
    bias = rpb[:, dr[:, None, None, :, None], dc[None, :, :, None, :]]
    bias = bias.reshape(NA_HEADS, rows, n_cb, NA_COLS, n_loc).astype(jnp.float32)

    def gather(t):
        g = t.reshape(bsz, rows, GRID_W, NA_HEADS, NA_HEAD_DIM)
        g = g[:, key_rows[:, None, :, None], key_cols[None, :, None, :]]
        return g.reshape(bsz, rows, n_cb, n_loc, NA_HEADS, NA_HEAD_DIM)

    q_b = q_l.reshape(bsz, rows, n_cb, NA_COLS, NA_HEADS, NA_HEAD_DIM)
    k_win, v_win = gather(k_l), gather(v_l)
    s_loc = jnp.einsum('brjqhd,brjkhd->bhrjqk', q_b, k_win).astype(jnp.float32) * scale + bias
    s_loc = jnp.where(in_win, s_loc, -jnp.inf)
    s_ctx = jnp.einsum('brjqhd,bkhd->bhrjqk', q_b, k_c).astype(jnp.float32) * scale
    p = jax.nn.softmax(jnp.concatenate([s_loc, s_ctx], axis=-1), axis=-1).astype(v_l.dtype)
    o = (jnp.einsum('bhrjqk,brjkhd->brjqhd', p[..., :n_loc], v_win)
         + jnp.einsum('bhrjqk,bkhd->brjqhd', p[..., n_loc:], v_c))
    y_lat = o.reshape(bsz, seq, D_MODEL) @ w_out
    y_ctx = None
    if ctx_out:
        q_c = heads(h_ctx @ w_qkv[:, :D_MODEL])
        s_cc = jnp.einsum('bqhd,bkhd->bhqk', q_c, k_c).astype(jnp.float32) * scale
        p_cc = jax.nn.softmax(s_cc, axis=-1).astype(v_c.dtype)
        o_c = jnp.einsum('bhqk,bkhd->bqhd', p_cc, v_c)
        y_ctx = o_c.reshape(bsz, h_ctx.shape[1], D_MODEL) @ w_out
    return y_ctx, y_lat


def setup_inputs(seed: int = 0) -> dict:
    key = jax.random.key(seed)
    keys = iter(jax.random.split(key, 32))
    f32 = jnp.float32

    def normal(shape, std):
        return jax.random.normal(next(keys), shape, f32) * std

    n_a, n_b, n_c, n_d = (_layers_of(m) for m in range(N_MIXERS))
    d = D_MODEL
    lam_shape = (n_c, 2, S5_GROUPS, S5_STATE)
    return {
        'x': normal((BATCH, SEQ, d), 1.0),
        'c': normal((BATCH, d), 1.0),
        'ctx': normal((BATCH, CTX_LEN, d), 1.0),
        'c_ctx': normal((d,), 1.0),
        'ada_w': normal((DEPTH, d, N_MOD * d), 0.5 * d ** -0.5),
        'ada_b': normal((DEPTH, N_MOD * d), 0.02),
        'norm_gains': 1.0 + normal((DEPTH, 4, d), 0.02),
        'mlp_w_in': normal((DEPTH, d, D_FF), d ** -0.5),
        'mlp_w_out': normal((DEPTH, D_FF, d), D_FF ** -0.5),
        'sc_w_in': normal((n_a, d, 3 * d), d ** -0.5),
        'sc_conv': normal((n_a, CONV_W, d), CONV_W ** -0.5),
        'sc_w_out': normal((n_a, d, d), d ** -0.5),
        'hg_w_in': normal((n_b, d, 5 * d), d ** -0.5),
        'hg_lower_bound': normal((DEPTH, d), 0.1),
        'hg_norm': 1.0 + normal((n_b, d), 0.02),
        'hg_w_out': normal((n_b, d, d), d ** -0.5),
        's5_lam_re': -0.5 + normal(lam_shape, 0.01),
        's5_lam_im': math.pi * jnp.arange(S5_STATE, dtype=f32) + normal(lam_shape, 0.01),
        's5_log_dt': jax.random.uniform(next(keys), (n_c, 2, S5_GROUPS), f32,
                                        math.log(S5_DT_MIN), math.log(S5_DT_MAX)),
        's5_b_re': normal((n_c, S5_GROUPS, S5_STATE, S5_GROUP), (2 * S5_GROUP) ** -0.5),
        's5_b_im': normal((n_c, S5_GROUPS, S5_STATE, S5_GROUP), (2 * S5_GROUP) ** -0.5),
        's5_c_re': normal((n_c, 2, S5_GROUPS, S5_GROUP, S5_STATE), (2 * S5_STATE) ** -0.5),
        's5_c_im': normal((n_c, 2, S5_GROUPS, S5_GROUP, S5_STATE), (2 * S5_STATE) ** -0.5),
        's5_d': normal((n_c, d), 1.0),
        's5_w_glu': normal((n_c, d, 2 * d), d ** -0.5),
        'na_w_qkv': normal((n_d, d, 3 * d), d ** -0.5),
        'na_rpb': normal((n_d, NA_HEADS, 2 * NA_ROWS - 1, 2 * NA_COLS - 1), 0.1),
        'na_w_out': normal((n_d, d, d), d ** -0.5),
    }


def reference(x, c, ctx, c_ctx, ada_w, ada_b, norm_gains, mlp_w_in, mlp_w_out,
              sc_w_in, sc_conv, sc_w_out, hg_w_in, hg_lower_bound, hg_norm, hg_w_out,
              s5_lam_re, s5_lam_im, s5_log_dt, s5_b_re, s5_b_im, s5_c_re, s5_c_im, s5_d, s5_w_glu,
              na_w_qkv, na_rpb, na_w_out):
    lb_all = jnp.cumsum(jax.nn.softmax(hg_lower_bound.astype(jnp.float32), axis=0), axis=0)
    lb_all = lb_all - lb_all[0]
    h_lat, h_ctx = x, ctx
    for i in range(DEPTH):
        kind, j = i % N_MIXERS, i // N_MIXERS
        ctx_out = i < DEPTH - 1
        mod_lat = [m[:, None, :] for m in jnp.split(jax.nn.silu(c) @ ada_w[i] + ada_b[i], N_MOD, axis=-1)]
        mod_ctx = jnp.split(jax.nn.silu(c_ctx) @ ada_w[i] + ada_b[i], N_MOD, axis=-1)
        g_pre, g_post, g_pre_ff, g_post_ff = norm_gains[i]
        a_lat = modulate(rmsnorm(h_lat, g_pre), mod_lat[0], mod_lat[1])
        a_ctx = modulate(rmsnorm(h_ctx, g_pre), mod_ctx[0], mod_ctx[1]) if (ctx_out or kind != 0) else None
        if kind == 0:
            y_lat = short_gated_conv(a_lat, sc_w_in[j], sc_conv[j], sc_w_out[j])
            y_ctx = short_gated_conv(a_ctx, sc_w_in[j], sc_conv[j], sc_w_out[j]) if ctx_out else None
        elif kind == 1:
            y_ctx, y_lat = hgrn2_mixer(a_ctx, a_lat, hg_w_in[j], lb_all[i], hg_norm[j], hg_w_out[j], ctx_out)
        elif kind == 2:
            y_ctx, y_lat = s5_mixer(a_ctx, a_lat, s5_lam_re[j], s5_lam_im[j], s5_log_dt[j], s5_b_re[j],
                                    s5_b_im[j], s5_c_re[j], s5_c_im[j], s5_d[j], s5_w_glu[j], ctx_out)
        else:
            y_ctx, y_lat = neighbourhood_attention(a_ctx, a_lat, na_w_qkv[j], na_rpb[j], na_w_out[j], ctx_out)
        h_lat = h_lat + mod_lat[2] * rmsnorm(y_lat.astype(h_lat.dtype), g_post)
        if ctx_out:
            h_ctx = h_ctx + mod_ctx[2] * rmsnorm(y_ctx.astype(h_ctx.dtype), g_post)
        f_lat = squared_relu_mlp(modulate(rmsnorm(h_lat, g_pre_ff), mod_lat[3], mod_lat[4]), mlp_w_in[i], mlp_w_out[i])
        h_lat = h_lat + mod_lat[5] * rmsnorm(f_lat.astype(h_lat.dtype), g_post_ff)
        if ctx_out:
            f_ctx = squared_relu_mlp(modulate(rmsnorm(h_ctx, g_pre_ff), mod_ctx[3], mod_ctx[4]), mlp_w_in[i], mlp_w_out[i])
            h_ctx = h_ctx + mod_ctx[5] * rmsnorm(f_ctx.astype(h_ctx.dtype), g_post_ff)
    return h_lat
```

```python
import numpy as np
from contextlib import ExitStack
import concourse.bass as bass
import concourse.mybir as mybir
from concourse.bass_utils import run_bass_kernel_spmd

F32 = mybir.dt.float32
BF16 = mybir.dt.bfloat16
ALU = mybir.AluOpType
AF = mybir.ActivationFunctionType
AX = mybir.AxisListType

D = 1024
L = 4096
LC = 256
NB = 2
DFF = 4096
NCORES = 8
EPS = 1e-6
TT = 512
ARENA_WORDS = 51968


class Buf:
    __slots__ = ("name", "lw", "rd", "slot", "cnt")

    def __init__(self, name=""):
        self.name = name
        self.lw = None
        self.rd = {}
        self.slot = None
        self.cnt = 0


class Op:
    __slots__ = ("eng", "idx", "meth", "args", "kw", "cw", "dw", "sig", "val", "key", "ord")


class Prog:
    ENGS = ("pe", "act", "dve", "pool", "sp")

    def __init__(self):
        self.ops = {e: [] for e in self.ENGS}
        self.keys = []
        self.slot_counts = []
        self.free_slots = []
        self.bar_ops = []
        self.bar_keys = []
        self.bar_pending = set()

    def add(self, eng, meth, args, kw=None, r=(), w=(), key=None):
        ops = self.ops[eng]
        op = Op()
        op.eng = eng
        op.idx = len(ops)
        op.meth = meth
        op.args = args
        op.kw = kw or {}
        op.cw = set()
        op.dw = {}
        op.sig = False
        op.val = 0
        op.key = None
        op.ord = 0
        isdma = key is not None

        def consider(o, raw):
            if o.key is not None:
                k = o.key
                if op.dw.get(k, 0) < o.ord:
                    op.dw[k] = o.ord
            elif isdma or o.eng != eng:
                op.cw.add(o)
            elif raw and eng != "pe" and op.idx - o.idx <= 3:
                op.cw.add(o)

        for b in r:
            if b.lw is not None:
                consider(b.lw, True)
        for b in w:
            if b.lw is not None:
                consider(b.lw, False)
            for o in b.rd.values():
                consider(o, False)
        if eng in self.bar_pending:
            self.bar_pending.discard(eng)
            for o in self.bar_ops:
                if o.eng != eng or isdma:
                    op.cw.add(o)
            for k, c in self.bar_keys:
                if op.dw.get(k, 0) < c:
                    op.dw[k] = c
        for b in r:
            b.rd[key if isdma else eng] = op
        for b in w:
            b.lw = op
            b.rd = {}
        if isdma:
            if key.slot is None:
                if self.free_slots:
                    key.slot = self.free_slots.pop()
                else:
                    key.slot = len(self.slot_counts)
                    self.slot_counts.append(0)
                self.keys.append(key)
            self.slot_counts[key.slot] += 1
            op.key = key.slot
            op.ord = self.slot_counts[key.slot]
        for o in op.cw:
            o.sig = True
        ops.append(op)
        return op

    def barrier(self):
        self.bar_ops = [self.ops[e][-1] for e in self.ENGS if self.ops[e] and self.ops[e][-1].key is None]
        self.bar_ops = []
        for e in self.ENGS:
            for o in reversed(self.ops[e]):
                if o.key is None:
                    self.bar_ops.append(o)
                    break
        self.bar_keys = [(sl, c) for sl, c in enumerate(self.slot_counts)]
        for k in self.keys:
            k.slot = None
        self.keys = []
        self.free_slots = list(range(len(self.slot_counts)))
        self.bar_pending = set(self.ENGS)

    def mm(self, out, lhsT, rhs, start, stop, r, w):
        return self.add("pe", "matmul", (out, lhsT, rhs), dict(start=start, stop=stop), r, w)

    def tr(self, out, in_, ident, r, w):
        return self.add("pe", "transpose", (out, in_, ident), None, r, w)

    def act(self, out, in_, func, r, w, bias=0.0, scale=1.0, eng="act"):
        return self.add(eng, "activation", (out, in_, func), dict(bias=bias, scale=scale), r, w)

    def dma(self, out, in_, r, w, key, eng="sp", **kw):
        return self.add(eng, "dma_start", (), dict(out=out, in_=in_, **kw), r, w, key=key)

    def emit(self, nc):
        with ExitStack() as es:
            esem = {e: es.enter_context(nc.semaphore("sem_" + e)) for e in self.ENGS}
            dsem = [es.enter_context(nc.semaphore("dsem%d" % i)) for i in range(len(self.slot_counts))]
            for e in self.ENGS:
                c = 0
                for op in self.ops[e]:
                    if op.key is None and op.sig:
                        c += 1
                        op.val = c
            block = es.enter_context(nc.Block())
            prog = self

            def run(e, eng):
                waited = {}
                for op in prog.ops[e]:
                    need = {}
                    for o in op.cw:
                        s = esem[o.eng]
                        if need.get(id(s), (None, 0))[1] < o.val:
                            need[id(s)] = (s, o.val)
                    for k, c in op.dw.items():
                        need[id(dsem[k])] = (dsem[k], 16 * c)
                    for sid, (s, v) in need.items():
                        if waited.get(sid, 0) < v:
                            eng.wait_ge(s, v)
                            waited[sid] = v
                    ins = getattr(eng, op.meth)(*op.args, **op.kw)
                    if op.key is not None:
                        ins.then_inc(dsem[op.key], 16)
                    elif op.sig:
                        ins.then_inc(esem[e], 1)
                if e == "sp":
                    for sl, c in enumerate(prog.slot_counts):
                        if waited.get(id(dsem[sl]), 0) < 16 * c:
                            eng.wait_ge(dsem[sl], 16 * c)

            @block.sync
            def _(eng):
                run("sp", eng)

            @block.scalar
            def _(eng):
                run("act", eng)

            @block.vector
            def _(eng):
                run("dve", eng)

            @block.gpsimd
            def _(eng):
                run("pool", eng)

            @block.tensor
            def _(eng):
                run("pe", eng)


class Arena:
    def __init__(self, t, nwords):
        self.t = t
        self.n = nwords
        self.off = 0

    def alloc(self, shape, dtype=F32, name=""):
        assert shape[0] == 128
        n = 1
        for s in shape[1:]:
            n *= s
        words = n if dtype == F32 else (n + 1) // 2
        words = (words + 7) // 8 * 8
        assert self.off + words <= self.n, ("arena overflow", name, self.off, words, self.n)
        v = self.t[:, self.off:self.off + words]
        self.off += words
        if dtype != F32:
            v = v.bitcast(dtype)
        v = v[:, 0:n]
        if len(shape) == 3:
            v = v.rearrange("p (a b) -> p a b", b=shape[2])
        elif len(shape) == 4:
            v = v.rearrange("p (a b c) -> p a b c", b=shape[2], c=shape[3])
        return v

    def mark(self):
        return self.off

    def reset(self, m):
        self.off = m


class K:
    pass


def dram_in(nc, name, shape, dtype=F32):
    return nc.dram_tensor(name, list(shape), dtype, kind="ExternalInput").ap()


INPUT_SHAPES = {
    "x": (NB, L, D), "ctx": (NB, LC, D), "crow": (128, 8, 3),
    "ada_w": (4, D, 6 * D), "ada_bT": (4, 128, 48), "gains": (4, 4, 128, 8),
    "mlp_w_in": (4, D, DFF), "mlp_w_out": (4, DFF, D),
    "ident": (128, 128),
    "sc_w_in": (D, 3 * D), "sc_convT": (128, 8, 3), "sc_w_out": (D, D),
    "hg_w_in": (D, 5 * D), "hlbT": (128, 8, 4), "hg_normT": (128, 8), "hg_w_out": (D, D),
    "hg_masks": (2, 128, 128), "hg_rmask": (128, 256), "hg_rowm": (128, 4),
    "s5_LR": (2, 128, 32), "s5_LI": (2, 128, 32), "s5_DT": (2, 128, 32),
    "s5_BRE": (128, 32, 16), "s5_BIM": (128, 32, 16), "s5_CRE": (2, 128, 32, 16), "s5_CIM": (2, 128, 32, 16),
    "s5_dT": (128, 8), "s5_w_glu": (D, 2 * D),
    "na_w_qkv": (D, 3 * D), "na_w_out": (D, D), "na_tbl": (16, 31, 64, 64), "na_valid": (3, 8, 128, 512),
}


def build(cfg):
    nc = bass.Bass("TRN2", target_bir_lowering=False)
    k = K()
    k.nc = nc
    k.cfg = cfg
    k.P = Prog()
    k.din = {n: dram_in(nc, n, s) for n, s in INPUT_SHAPES.items()}
    k.dbuf = {n: Buf("d_" + n) for n in INPUT_SHAPES}
    k.out = nc.dram_tensor("out", [NB, L, D], F32, kind="ExternalOutput").ap()
    k.out_buf = Buf("d_out")
    if cfg.get("dump_ctx"):
        k.dbgc = nc.dram_tensor("dbgc", [NB, D, LC], F32, kind="ExternalOutput").ap()
    k.hT = nc.dram_tensor("hT", [NB, D, L], F32, kind="Internal").ap()
    k.hcT = nc.dram_tensor("hcT", [NB, D, LC], F32, kind="Internal").ap()
    k.hT_buf = [Buf("hT%d" % b) for b in range(NB)]
    k.hcT_buf = [Buf("hcT%d" % b) for b in range(NB)]
    with ExitStack() as es:
        arena_t = es.enter_context(nc.sbuf_tensor("arena", [128, ARENA_WORDS], F32))
        ps_t = es.enter_context(nc.psum_tensor("ps", [128, 4096], F32))
        k.A = Arena(arena_t, ARENA_WORDS)
        k.ps = [ps_t[:, i * 512:(i + 1) * 512] for i in range(8)]
        k.psb = [Buf("ps%d" % i) for i in range(8)]
        setup_consts(k)
        stage_mod(k)
        stage_in(k)
        for layer in cfg.get("layers", range(4)):
            if "mix" in cfg.get("parts", ("mix", "mlp")):
                if layer % 4 == 0:
                    stage_conv(k, layer, layer // 4)
                elif layer % 4 == 1:
                    stage_hgrn(k, layer, layer // 4)
                elif layer % 4 == 2:
                    stage_s5(k, layer, layer // 4)
                else:
                    stage_na(k, layer, layer // 4)
            if "mlp" in cfg.get("parts", ("mix", "mlp")):
                stage_mlp(k, layer)
        stage_out(k)
        k.P.emit(nc)
    return nc


def setup_consts(k):
    P, A = k.P, k.A
    k.ident = A.alloc([128, 128], F32, "ident")
    k.ident_b = Buf("ident")
    P.dma(k.ident, k.din["ident"], [k.dbuf["ident"]], [k.ident_b], key=k.ident_b)
    k.ones_bf = A.alloc([128, 128], BF16, "ones")
    k.ones_b = Buf("ones")
    P.add("dve", "memset", (k.ones_bf, 1.0), None, [], [k.ones_b])
    k.identb = A.alloc([128, 128], BF16, "identb")
    k.identb_b = Buf("identb")
    P.add("dve", "tensor_copy", (k.identb, k.ident), None, [k.ident_b], [k.identb_b])
    k.mod = [A.alloc([128, 48, 3], F32, "mod%d" % i) for i in range(4)]
    k.mod_b = [Buf("mod%d" % i) for i in range(4)]
    k.gains = A.alloc([128, 16, 8], F32, "gains")
    k.gains_b = Buf("gains")
    P.dma(k.gains, k.din["gains"].rearrange("l w p c -> p (l w) c"), [k.dbuf["gains"]], [k.gains_b], key=k.gains_b)
    k.gs = [[A.alloc([128, 8, 3], F32) for _ in range(2)] for _ in range(4)]
    k.gg = [[A.alloc([128, 8, 3], F32) for _ in range(2)] for _ in range(4)]
    k.gsg_b = [Buf("gsg%d" % i) for i in range(4)]


def stage_mod(k):
    P, A, nc = k.P, k.A, k.nc
    m0 = A.mark()
    crow = A.alloc([128, 8, 3], F32, "crow")
    crow_b = Buf("crow")
    P.dma(crow, k.din["crow"], [k.dbuf["crow"]], [crow_b], key=crow_b)
    sc = A.alloc([128, 8, 3], F32, "silu_c")
    sc_b = Buf("silu_c")
    P.act(sc, crow, AF.Silu, [crow_b], [sc_b])
    adab = A.alloc([128, 4, 48], F32, "adab")
    adab_b = Buf("adab")
    P.dma(adab, k.din["ada_bT"].rearrange("l p c -> p l c"), [k.dbuf["ada_bT"]], [adab_b], key=adab_b)
    NBLK = 12
    wbuf = [A.alloc([128, 8, 512], F32, "adaw%d" % i) for i in range(3)]
    wbuf_b = [Buf("adaw%d" % i) for i in range(3)]
    it = 0
    for layer in range(4):
        wsrc = k.din["ada_w"][layer].rearrange("(c p) n -> p c n", p=128)
        for blk in range(NBLK):
            s = it % 3
            P.dma(wbuf[s], wsrc[:, :, blk * 512:(blk + 1) * 512], [k.dbuf["ada_w"]], [wbuf_b[s]], key=wbuf_b[s])
            pi = it % 2
            for j in range(4):
                for c in range(8):
                    P.mm(k.ps[pi][:, j * 8:j * 8 + 3], wbuf[s][:, c, j * 128:(j + 1) * 128], sc[:, c, :],
                         c == 0, c == 7, [wbuf_b[s], sc_b], [k.psb[pi]])
            for j in range(4):
                col = blk * 4 + j
                P.act(k.mod[layer][:, col, :], k.ps[pi][:, j * 8:j * 8 + 3], AF.Identity,
                      [k.psb[pi], adab_b], [k.mod_b[layer]], bias=adab[:, layer, col:col + 1])
            it += 1
        for which in range(2):
            g_pre = k.gains[:, layer * 4 + 2 * which, :]
            g_post = k.gains[:, layer * 4 + 2 * which + 1, :]
            scale = k.mod[layer][:, (3 * which + 1) * 8:(3 * which + 2) * 8, :]
            gate = k.mod[layer][:, (3 * which + 2) * 8:(3 * which + 3) * 8, :]
            gs, gg = k.gs[layer][which], k.gg[layer][which]
            for rrow in range(3):
                P.add("dve", "scalar_tensor_tensor", (), dict(out=gs[:, :, rrow], in0=scale[:, :, rrow], scalar=1.0,
                      in1=g_pre, op0=ALU.add, op1=ALU.mult), [k.mod_b[layer], k.gains_b], [k.gsg_b[layer]])
                P.add("dve", "tensor_tensor", (), dict(out=gg[:, :, rrow], in0=gate[:, :, rrow], in1=g_post,
                      op=ALU.mult), [k.mod_b[layer], k.gains_b], [k.gsg_b[layer]])
    A.reset(m0)
    P.barrier()


def mod_vec(k, layer, kind, c, row):
    return k.mod[layer][:, kind * 8 + c, row:row + 1]


def seq_tiles(tt=TT):
    out = []
    for b in range(NB):
        for t in range(max(1, LC // tt)):
            out.append((True, b, t, t * tt, min(tt, LC)))
        for t in range(L // tt):
            out.append((False, b, t, t * tt, tt))
    return out


def h_ap(k, isctx, b, t0, n):
    src = k.hcT if isctx else k.hT
    return src[b].rearrange("(c p) t -> p c t", p=128)[:, :, t0:t0 + n]


def h_buf(k, isctx, b):
    return (k.hcT_buf if isctx else k.hT_buf)[b]


def stage_in(k):
    P, A = k.P, k.A
    m0 = A.mark()
    xin = [A.alloc([128, 4, D], F32, "xin%d" % i) for i in range(2)]
    xin_b = [Buf("xin%d" % i) for i in range(2)]
    hto = [A.alloc([128, 8, TT], F32, "hto%d" % i) for i in range(2)]
    hto_b = [Buf("hto%d" % i) for i in range(2)]
    it = 0
    pc = 0
    for (isctx, b, t, t0, n) in seq_tiles():
        s = it % 2
        nsub = n // 128
        src = (k.din["ctx"] if isctx else k.din["x"])[b, t0:t0 + n, :].rearrange("(j p) d -> p j d", p=128)
        P.dma(xin[s][:, 0:nsub, :], src, [k.dbuf["ctx" if isctx else "x"]], [xin_b[s]], key=xin_b[s])
        for c in range(8):
            pi = pc % 4
            pc += 1
            for j in range(nsub):
                P.tr(k.ps[pi][:, j * 128:(j + 1) * 128], xin[s][:, j, c * 128:(c + 1) * 128], k.ident,
                     [xin_b[s], k.ident_b], [k.psb[pi]])
            eng = "act" if c % 2 == 0 else "dve"
            if eng == "act":
                P.act(hto[s][:, c, 0:n], k.ps[pi][:, 0:n], AF.Copy, [k.psb[pi]], [hto_b[s]])
            else:
                P.add("dve", "tensor_copy", (hto[s][:, c, 0:n], k.ps[pi][:, 0:n]), None, [k.psb[pi]], [hto_b[s]])
        P.dma(h_ap(k, isctx, b, t0, n), hto[s][:, :, 0:n], [hto_b[s]], [h_buf(k, isctx, b)], key=hto_b[s], eng="pool")
        it += 1
    A.reset(m0)
    P.barrier()


def stage_out(k):
    P, A = k.P, k.A
    m0 = A.mark()
    hin = [A.alloc([128, 8, TT], F32, "hin%d" % i) for i in range(2)]
    hin_b = [Buf("hin%d" % i) for i in range(2)]
    xo = [A.alloc([128, 4, D], F32, "xo%d" % i) for i in range(2)]
    xo_b = [Buf("xo%d" % i) for i in range(2)]
    it = 0
    pc = 0
    for (isctx, b, t, t0, n) in seq_tiles():
        if isctx:
            continue
        s = it % 2
        P.dma(hin[s], h_ap(k, False, b, t0, n), [k.hT_buf[b]], [hin_b[s]], key=hin_b[s])
        for j in range(4):
            for half in range(2):
                pi = pc % 4
                pc += 1
                for cc in range(4):
                    c = half * 4 + cc
                    P.tr(k.ps[pi][:, cc * 128:(cc + 1) * 128], hin[s][:, c, j * 128:(j + 1) * 128], k.ident,
                         [hin_b[s], k.ident_b], [k.psb[pi]])
                dst = xo[s][:, j, half * 512:(half + 1) * 512]
                if half == 0:
                    P.act(dst, k.ps[pi], AF.Copy, [k.psb[pi]], [xo_b[s]])
                else:
                    P.add("dve", "tensor_copy", (dst, k.ps[pi]), None, [k.psb[pi]], [xo_b[s]])
        dst = k.out[b, t0:t0 + n, :].rearrange("(j p) d -> p j d", p=128)
        P.dma(dst, xo[s], [xo_b[s]], [k.out_buf], key=xo_b[s], eng="pool")
        it += 1
    if k.cfg.get("dump_ctx"):
        db = Buf("dbgc")
        for b in range(NB):
            P.dma(hin[0][:, :, 0:LC], h_ap(k, True, b, 0, LC), [k.hcT_buf[b]], [hin_b[0]], key=hin_b[0])
            P.dma(k.dbgc[b].rearrange("(c p) t -> p c t", p=128), hin[0][:, :, 0:LC], [hin_b[0]], [db], key=hin_b[0])
    A.reset(m0)


def rms_rstd(k, src, src_b, n, sq, sq_b, R, R_b, psi):
    P = k.P
    P.act(sq[:, :, 0:n], src[:, :, 0:n], AF.Square, [src_b], [sq_b])
    for c in range(8):
        P.mm(k.ps[psi][:, 0:n], k.ones_bf, sq[:, c, 0:n], c == 0, c == 7, [k.ones_b, sq_b], [k.psb[psi]])
    P.act(R[:, 0:n], k.ps[psi][:, 0:n], AF.Sqrt, [k.psb[psi]], [R_b], bias=EPS, scale=1.0 / D)
    P.add("dve", "reciprocal", (), dict(out=R[:, 0:n], in_=R[:, 0:n]), [R_b], [R_b])


def load_weight_bf16(k, dst, dst_b, src, src_b, nk, ncol, stg, stg_b, cast_engs=("pool", "dve")):
    P = k.P
    CW = stg[0].shape[-1]
    srcv = src.rearrange("(c p) n -> p c n", p=128)
    it = 0
    for c in range(nk):
        for c0 in range(0, ncol, CW):
            s = it % len(stg)
            w = min(CW, ncol - c0)
            P.dma(stg[s][:, 0:w], srcv[:, c, c0:c0 + w], [src_b], [stg_b[s]], key=stg_b[s])
            eng = cast_engs[it % len(cast_engs)]
            if eng == "act":
                P.act(dst[:, c, c0:c0 + w], stg[s][:, 0:w], AF.Copy, [stg_b[s]], [dst_b])
            else:
                P.add(eng, "tensor_copy", (dst[:, c, c0:c0 + w], stg[s][:, 0:w]), None, [stg_b[s]], [dst_b])
            it += 1


class NormBufs:
    def __init__(self, k, n, name=""):
        A = k.A
        self.SQ = A.alloc([128, 8, n], BF16, "SQ" + name)
        self.SQ_b = Buf("SQ")
        self.R = A.alloc([128, n], F32, "R" + name)
        self.R_b = Buf("R")
        self.T1 = [A.alloc([128, n], F32, "T1%d" % i) for i in range(2)]
        self.T1_b = [Buf("T1%d" % i) for i in range(2)]


def prenorm_mod(k, layer, which, Hs, Hsb, n, row, W, Aa, Aa_b, psi=7):
    P = k.P
    gs = k.gs[layer][which]
    rms_rstd(k, Hs, Hsb, n, W.SQ, W.SQ_b, W.R, W.R_b, psi)
    for c in range(8):
        T1, T1_b = W.T1[c % 2], W.T1_b[c % 2]
        P.add("dve", "scalar_tensor_tensor", (), dict(out=T1[:, 0:n], in0=Hs[:, c, 0:n], scalar=gs[:, c, row:row + 1],
              in1=W.R[:, 0:n], op0=ALU.mult, op1=ALU.mult), [Hsb, W.R_b, k.gsg_b[layer]], [T1_b])
        P.act(Aa[:, c, 0:n], T1[:, 0:n], AF.Identity, [T1_b, k.mod_b[layer]], [Aa_b],
              bias=mod_vec(k, layer, 3 * which, c, row))


def postnorm_res(k, layer, which, Fo, Fo_b, Hs, Hsb, n, row, W, psi=7):
    P = k.P
    gg = k.gg[layer][which]
    rms_rstd(k, Fo, Fo_b, n, W.SQ, W.SQ_b, W.R, W.R_b, psi)
    for c in range(8):
        T1, T1_b = W.T1[c % 2], W.T1_b[c % 2]
        P.add("dve", "scalar_tensor_tensor", (), dict(out=T1[:, 0:n], in0=Fo[:, c, 0:n], scalar=gg[:, c, row:row + 1],
              in1=W.R[:, 0:n], op0=ALU.mult, op1=ALU.mult), [Fo_b, W.R_b, k.gsg_b[layer]], [T1_b])
        P.add("pool", "tensor_tensor", (), dict(out=Hs[:, c, 0:n], in0=Hs[:, c, 0:n], in1=T1[:, 0:n], op=ALU.add),
              [Hsb, T1_b], [Hsb])


def evac(k, dst, src, i, r, w):
    if i % 2 == 0:
        k.P.act(dst, src, AF.Copy, r, w)
    else:
        k.P.add("dve", "tensor_copy", (dst, src), None, r, w)


def fm_ap(t, t0, n):
    return t.rearrange("(c p) t -> p c t", p=128)[:, :, t0:t0 + n]


def stage_conv(k, layer, j):
    P, A, nc = k.P, k.A, k.nc
    m0 = A.mark()
    Bs = [nc.dram_tensor("cv_b%d" % b, [D, L], BF16, kind="Internal").ap() for b in range(NB)]
    Bc = [nc.dram_tensor("cv_bc%d" % b, [D, LC], BF16, kind="Internal").ap() for b in range(NB)]
    Us = [nc.dram_tensor("cv_u%d" % b, [D, L + 2], F32, kind="Internal").ap() for b in range(NB)]
    Uc = [nc.dram_tensor("cv_uc%d" % b, [D, LC + 2], F32, kind="Internal").ap() for b in range(NB)]
    sB = [[Buf("cvB"), Buf("cvBc")] for b in range(NB)]
    sU = [[Buf("cvU"), Buf("cvUc")] for b in range(NB)]
    cw = A.alloc([128, 8, 3], F32, "convw")
    cw_b = Buf("convw")
    P.dma(cw, k.din["sc_convT"], [k.dbuf["sc_convT"]], [cw_b], key=cw_b)
    Z = A.alloc([128, 8, 1], F32, "zero")
    Z_b = Buf("zero")
    P.add("dve", "memset", (Z, 0.0), None, [], [Z_b])
    for b in range(NB if not k.cfg.get('nohalo') else 0):
        for (t, ln, sb) in ((Us[b], L, sU[b][0]), (Uc[b], LC, sU[b][1])):
            v = t.rearrange("(c p) t -> p c t", p=128)
            P.dma(v[:, :, 0:1], Z, [Z_b], [sb], key=Z_b, allow_slow_non_contiguous=True)
            P.dma(v[:, :, ln + 1:ln + 2], Z, [Z_b], [sb], key=Z_b, allow_slow_non_contiguous=True)
    w2 = A.alloc([128, 8, D], BF16, "cw2")
    w2_b = Buf("cw2")
    m1 = A.mark()
    w1 = A.alloc([128, 8, 3 * D], BF16, "cw1")
    w1_b = Buf("cw1")
    stg = [A.alloc([128, 1024], F32, "stg%d" % i) for i in range(2)]
    stg_b = [Buf("stg%d" % i) for i in range(2)]
    load_weight_bf16(k, w1, w1_b, k.din["sc_w_in"], k.dbuf["sc_w_in"], 8, 3 * D, stg, stg_b)
    load_weight_bf16(k, w2, w2_b, k.din["sc_w_out"], k.dbuf["sc_w_out"], 8, D, stg, stg_b)
    H = [A.alloc([128, 8, TT], F32, "H%d" % i) for i in range(2)]
    H_b = [Buf("H%d" % i) for i in range(2)]
    W = NormBufs(k, TT)
    Aa = A.alloc([128, 8, TT], BF16, "Aa")
    Aa_b = Buf("Aa")
    Bt = [A.alloc([128, 8, TT], BF16, "Bt%d" % i) for i in range(2)]
    Bt_b = [Buf("Bt%d" % i) for i in range(2)]
    Ct = [A.alloc([128, TT], F32, "Ct%d" % i) for i in range(2)]
    Ct_b = [Buf("Ct%d" % i) for i in range(2)]
    Ut = [A.alloc([128, 8, TT + 2], F32, "Ut%d" % i) for i in range(2)]
    Ut_b = [Buf("Ut%d" % i) for i in range(2)]
    tiles = seq_tiles()[:k.cfg.get('maxtiles', 100)]

    def load1(i):
        isctx, b, t, t0, n = tiles[i]
        P.dma(H[i % 2][:, :, 0:n], h_ap(k, isctx, b, t0, n), [h_buf(k, isctx, b)], [H_b[i % 2]], key=H_b[i % 2])

    load1(0)
    pc = 0
    for i, (isctx, b, t, t0, n) in enumerate(tiles):
        s = i % 2
        row = 2 if isctx else b
        if i + 1 < len(tiles):
            load1(i + 1)
        prenorm_mod(k, layer, 0, H[s], H_b[s], n, row, W, Aa, Aa_b)
        for c in range(8):
            pi = pc % 6
            pc += 1
            for kk in range(8):
                P.mm(k.ps[pi][:, 0:n], w1[:, kk, c * 128:(c + 1) * 128], Aa[:, kk, 0:n], kk == 0, kk == 7,
                     [w1_b, Aa_b], [k.psb[pi]])
            evac(k, Bt[s][:, c, 0:n], k.ps[pi][:, 0:n], c, [k.psb[pi]], [Bt_b[s]])
            pi = pc % 6
            pc += 1
            for kk in range(8):
                P.mm(k.ps[pi][:, 0:n], w1[:, kk, D + c * 128:D + (c + 1) * 128], Aa[:, kk, 0:n], kk == 0, kk == 7,
                     [w1_b, Aa_b], [k.psb[pi]])
            cs = c % 2
            P.act(Ct[cs][:, 0:n], k.ps[pi][:, 0:n], AF.Copy, [k.psb[pi]], [Ct_b[cs]])
            pi = pc % 6
            pc += 1
            for kk in range(8):
                P.mm(k.ps[pi][:, 0:n], w1[:, kk, 2 * D + c * 128:2 * D + (c + 1) * 128], Aa[:, kk, 0:n], kk == 0, kk == 7,
                     [w1_b, Aa_b], [k.psb[pi]])
            P.add("dve", "tensor_tensor", (), dict(out=Ut[s][:, c, 0:n], in0=k.ps[pi][:, 0:n], in1=Ct[cs][:, 0:n],
                  op=ALU.mult), [k.psb[pi], Ct_b[cs]], [Ut_b[s]])
        bdst = (Bc if isctx else Bs)[b]
        udst = (Uc if isctx else Us)[b]
        P.dma(fm_ap(bdst, t0, n), Bt[s][:, :, 0:n], [Bt_b[s]], [sB[b][1 if isctx else 0]], key=Bt_b[s], eng="pool")
        P.dma(fm_ap(udst, t0 + 1, n), Ut[s][:, :, 0:n], [Ut_b[s]], [sU[b][1 if isctx else 0]], key=Ut_b[s], eng="pool")

    if k.cfg.get('p1only'):
        A.reset(m0)
        P.barrier()
        return
    P.barrier()
    A.reset(m1)
    H = [A.alloc([128, 8, TT], F32, "H%d" % i) for i in range(2)]
    H_b = [Buf("H%d" % i) for i in range(2)]
    W = NormBufs(k, TT)
    Bt = [A.alloc([128, 8, TT], BF16, "Bt%d" % i) for i in range(2)]
    Bt_b = [Buf("Bt%d" % i) for i in range(2)]
    Ut = [A.alloc([128, 8, TT + 2], F32, "Ut%d" % i) for i in range(2)]
    Ut_b = [Buf("Ut%d" % i) for i in range(2)]
    Gt = A.alloc([128, 8, TT], BF16, "Gt")
    Gt_b = Buf("Gt")
    Fo = A.alloc([128, 8, TT], F32, "Fo")
    Fo_b = Buf("Fo")
    V1 = [A.alloc([128, TT], F32, "V1%d" % i) for i in range(2)]
    V1_b = [Buf("V1%d" % i) for i in range(2)]

    def load2(i):
        isctx, b, t, t0, n = tiles[i]
        s = i % 2
        P.dma(H[s][:, :, 0:n], h_ap(k, isctx, b, t0, n), [h_buf(k, isctx, b)], [H_b[s]], key=H_b[s])
        P.dma(Ut[s][:, :, 0:n + 2], fm_ap((Uc if isctx else Us)[b], t0, n + 2), [sU[b][1 if isctx else 0]], [Ut_b[s]], key=Ut_b[s])
        P.dma(Bt[s][:, :, 0:n], fm_ap((Bc if isctx else Bs)[b], t0, n), [sB[b][1 if isctx else 0]], [Bt_b[s]], key=Bt_b[s])

    load2(0)
    for i, (isctx, b, t, t0, n) in enumerate(tiles):
        s = i % 2
        row = 2 if isctx else b
        if i + 1 < len(tiles):
            load2(i + 1)
        for c in range(8):
            v = V1[c % 2]
            vb = V1_b[c % 2]
            P.act(v[:, 0:n], Ut[s][:, c, 0:n], AF.Identity, [Ut_b[s], cw_b], [vb], scale=cw[:, c, 0:1])
            P.add("dve", "scalar_tensor_tensor", (), dict(out=v[:, 0:n], in0=Ut[s][:, c, 1:n + 1], scalar=cw[:, c, 1:2],
                  in1=v[:, 0:n], op0=ALU.mult, op1=ALU.add), [Ut_b[s], cw_b, vb], [vb])
            P.add("dve", "scalar_tensor_tensor", (), dict(out=v[:, 0:n], in0=Ut[s][:, c, 2:n + 2], scalar=cw[:, c, 2:3],
                  in1=v[:, 0:n], op0=ALU.mult, op1=ALU.add), [Ut_b[s], cw_b, vb], [vb])
            P.add("pool", "tensor_tensor", (), dict(out=Gt[:, c, 0:n], in0=v[:, 0:n], in1=Bt[s][:, c, 0:n], op=ALU.mult),
                  [vb, Bt_b[s]], [Gt_b])
        for c in range(8):
            pi = pc % 6
            pc += 1
            for kk in range(8):
                P.mm(k.ps[pi][:, 0:n], w2[:, kk, c * 128:(c + 1) * 128], Gt[:, kk, 0:n], kk == 0, kk == 7,
                     [w2_b, Gt_b], [k.psb[pi]])
            evac(k, Fo[:, c, 0:n], k.ps[pi][:, 0:n], c, [k.psb[pi]], [Fo_b])
        postnorm_res(k, layer, 0, Fo, Fo_b, H[s], H_b[s], n, row, W)
        P.dma(h_ap(k, isctx, b, t0, n), H[s][:, :, 0:n], [H_b[s]], [h_buf(k, isctx, b)], key=H_b[s], eng="pool")
    A.reset(m0)
    P.barrier()


def stage_hgrn(k, layer, j):
    P, A, nc = k.P, k.A, k.nc
    GT = 256
    LT = LC + L
    NT = LT // GT
    m0 = A.mark()
    Qs = [nc.dram_tensor("hg_q%d" % b, [D, LT], BF16, kind="Internal").ap() for b in range(NB)]
    Gs = [nc.dram_tensor("hg_g%d" % b, [D, LT], BF16, kind="Internal").ap() for b in range(NB)]
    Kd = [[nc.dram_tensor("hg_k%d_%d" % (b, d_), [D, LT], BF16, kind="Internal").ap() for d_ in range(2)] for b in range(NB)]
    LFd = [[nc.dram_tensor("hg_lf%d_%d" % (b, d_), [D, LT], F32, kind="Internal").ap() for d_ in range(2)] for b in range(NB)]
    Vs = [nc.dram_tensor("hg_v%d" % b, [LT, D], BF16, kind="Internal").ap() for b in range(NB)]
    Of = [nc.dram_tensor("hg_of%d" % b, [D, LT], F32, kind="Internal").ap() for b in range(NB)]
    sQ = [Buf("hgQ") for b in range(NB)]
    sG = [Buf("hgG") for b in range(NB)]
    sK = [[Buf("hgK") for d_ in range(2)] for b in range(NB)]
    sLF = [[Buf("hgLF") for d_ in range(2)] for b in range(NB)]
    sV = [Buf("hgV") for b in range(NB)]
    sOf = [Buf("hgOf") for b in range(NB)]

    def tile_src(b, t):
        if t == 0:
            return True, 0
        return False, (t - 1) * GT

    hlb = A.alloc([128, 8, 4], F32, "hlb")
    hlb_b = Buf("hlb")
    P.dma(hlb, k.din["hlbT"], [k.dbuf["hlbT"]], [hlb_b], key=hlb_b)
    gno = A.alloc([128, 8], F32, "gnorm")
    gno_b = Buf("gnorm")
    P.dma(gno, k.din["hg_normT"], [k.dbuf["hg_normT"]], [gno_b], key=gno_b)
    lbv = A.alloc([128, 8], F32, "lbv")
    oml = A.alloc([128, 8], F32, "oml")
    lb_b = Buf("lb")
    ssum = A.alloc([128, 8], F32, "ssum")
    P.act(hlb, hlb, AF.Exp, [hlb_b], [hlb_b])
    P.add("dve", "tensor_reduce", (), dict(out=ssum, in_=hlb, axis=AX.X, op=ALU.add), [hlb_b], [lb_b])
    P.add("dve", "reciprocal", (), dict(out=ssum, in_=ssum), [lb_b], [lb_b])
    P.add("dve", "tensor_tensor", (), dict(out=lbv, in0=hlb[:, :, 1], in1=ssum, op=ALU.mult), [hlb_b, lb_b], [lb_b])
    for i_ in range(2, layer + 1):
        P.add("dve", "scalar_tensor_tensor", (), dict(out=lbv, in0=hlb[:, :, i_], scalar=1.0, in1=ssum, op0=ALU.mult,
              op1=ALU.mult), [hlb_b, lb_b], [lb_b])
    P.add("dve", "tensor_scalar", (), dict(out=oml, in0=lbv, scalar1=-1.0, scalar2=1.0, op0=ALU.mult, op1=ALU.add),
          [lb_b], [lb_b])
    cmask = A.alloc([128, 2, 128], F32, "cmask")
    cmask_b = Buf("cmask")
    P.dma(cmask, k.din["hg_masks"].rearrange("a p t -> p a t"), [k.dbuf["hg_masks"]], [cmask_b], key=cmask_b)
    rmask = A.alloc([128, GT], F32, "rmask")
    rmask_b = Buf("rmask")
    P.dma(rmask, k.din["hg_rmask"], [k.dbuf["hg_rmask"]], [rmask_b], key=rmask_b)
    rowm = A.alloc([128, 4], F32, "rowm")
    rowm_b = Buf("rowm")
    P.dma(rowm, k.din["hg_rowm"], [k.dbuf["hg_rowm"]], [rowm_b], key=rowm_b)
    wo = A.alloc([128, 8, D], BF16, "hwo")
    wo_b = Buf("hwo")
    m1 = A.mark()

    w1 = A.alloc([128, 8, 5 * D], BF16, "hw1")
    w1_b = Buf("hw1")
    stg = [A.alloc([128, 1024], F32, "stg%d" % i) for i in range(2)]
    stg_b = [Buf("stg%d" % i) for i in range(2)]
    load_weight_bf16(k, w1, w1_b, k.din["hg_w_in"], k.dbuf["hg_w_in"], 8, 5 * D, stg, stg_b)
    load_weight_bf16(k, wo, wo_b, k.din["hg_w_out"], k.dbuf["hg_w_out"], 8, D, stg, stg_b)
    H = [A.alloc([128, 8, GT], F32, "H%d" % i) for i in range(2)]
    H_b = [Buf("H%d" % i) for i in range(2)]
    W = NormBufs(k, GT)
    Aa = A.alloc([128, 8, GT], BF16, "Aa")
    Aa_b = Buf("Aa")
    Qt = A.alloc([128, 8, GT], BF16, "Qt")
    Qt_b = Buf("Qt")
    Gt = A.alloc([128, 8, GT], BF16, "Gt")
    Gt_b = Buf("Gt")
    SG = [A.alloc([128, GT], F32, "SG%d" % i) for i in range(2)]
    SG_b = [Buf("SG%d" % i) for i in range(2)]
    Ft = A.alloc([128, 16, GT], F32, "Ft")
    Ft_b = Buf("Ft")
    Kt = A.alloc([128, 16, GT], BF16, "Kt")
    Kt_b = Buf("Kt")
    LFt = A.alloc([128, 16, GT], F32, "LFt")
    LFt_b = Buf("LFt")
    Vt = A.alloc([128, 2, D], BF16, "Vt")
    Vt_b = Buf("Vt")
    tl = [(b, t) for b in range(NB) for t in range(NT)][:k.cfg.get("maxtiles", 1000)]

    def loadH(i, Hs, Hb):
        b, t = tl[i]
        isctx, t0 = tile_src(b, t)
        P.dma(Hs[i % 2], h_ap(k, isctx, b, t0, GT), [h_buf(k, isctx, b)], [Hb[i % 2]], key=Hb[i % 2])

    loadH(0, H, H_b)
    pc = 0
    for i, (b, t) in enumerate(tl):
        s = i % 2
        isctx, t0 = tile_src(b, t)
        row = 2 if isctx else b
        g0 = t * GT
        if i + 1 < len(tl):
            loadH(i + 1, H, H_b)
        prenorm_mod(k, layer, 0, H[s], H_b[s], GT, row, W, Aa, Aa_b)
        for c in range(8):
            pi = pc % 6
            pc += 1
            for kk in range(8):
                P.mm(k.ps[pi][:, 0:GT], w1[:, kk, c * 128:(c + 1) * 128], Aa[:, kk, :], kk == 0, kk == 7,
                     [w1_b, Aa_b], [k.psb[pi]])
            evac(k, Qt[:, c, :], k.ps[pi][:, 0:GT], c, [k.psb[pi]], [Qt_b])
        for c in range(24):
            pi = pc % 6
            pc += 1
            col0 = 2 * D + c * 128
            for kk in range(8):
                P.mm(k.ps[pi][:, 0:GT], w1[:, kk, col0:col0 + 128], Aa[:, kk, :], kk == 0, kk == 7,
                     [w1_b, Aa_b], [k.psb[pi]])
            sg, sgb = SG[c % 2], SG_b[c % 2]
            P.act(sg, k.ps[pi][:, 0:GT], AF.Sigmoid, [k.psb[pi]], [sgb])
            if c < 8:
                P.add("dve", "tensor_tensor", (), dict(out=Gt[:, c, :], in0=k.ps[pi][:, 0:GT], in1=sg, op=ALU.mult),
                      [k.psb[pi], sgb], [Gt_b])
            else:
                cc = (c - 8) % 8
                P.add("dve", "tensor_scalar", (), dict(out=Ft[:, c - 8, :], in0=sg, scalar1=oml[:, cc:cc + 1],
                      scalar2=lbv[:, cc:cc + 1], op0=ALU.mult, op1=ALU.add), [sgb, lb_b], [Ft_b])
        P.add("pool", "tensor_scalar", (), dict(out=Kt, in0=Ft, scalar1=-1.0, scalar2=1.0, op0=ALU.mult, op1=ALU.add),
              [Ft_b], [Kt_b])
        P.act(LFt, Ft, AF.Ln, [Ft_b], [LFt_b])
        for blk in range(2):
            for half in range(2):
                pi = pc % 6
                pc += 1
                for kk in range(8):
                    P.mm(k.ps[pi], Aa[:, kk, blk * 128:(blk + 1) * 128],
                         w1[:, kk, D + half * 512:D + (half + 1) * 512], kk == 0, kk == 7, [w1_b, Aa_b], [k.psb[pi]])
                evac(k, Vt[:, blk, half * 512:(half + 1) * 512], k.ps[pi], blk * 2 + half, [k.psb[pi]], [Vt_b])
        P.dma(fm_ap(Qs[b], g0, GT), Qt, [Qt_b], [sQ[b]], key=Qt_b, eng="pool")
        P.dma(fm_ap(Gs[b], g0, GT), Gt, [Gt_b], [sG[b]], key=Gt_b, eng="pool")
        for d_ in range(2):
            P.dma(fm_ap(Kd[b][d_], g0, GT), Kt[:, d_ * 8:(d_ + 1) * 8, :], [Kt_b], [sK[b][d_]], key=Kt_b, eng="pool")
            P.dma(fm_ap(LFd[b][d_], g0, GT), LFt[:, d_ * 8:(d_ + 1) * 8, :], [LFt_b], [sLF[b][d_]], key=LFt_b, eng="pool")
        P.dma(Vs[b][g0:g0 + GT, :].rearrange("(j p) d -> p j d", p=128), Vt, [Vt_b], [sV[b]], key=Vt_b, eng="pool")
    P.barrier()
    A.reset(m1)
    if k.cfg.get("p1only"):
        A.reset(m0)
        return

    S32 = A.alloc([128, 8, 128], F32, "S32")
    Sbf = A.alloc([128, 8, 128], BF16, "Sbf")
    S_b = [Buf("S32_%d" % h) for h in range(8)]
    Sbf_b = [Buf("Sbf_%d" % h) for h in range(8)]
    Qt = [A.alloc([128, 8, GT], BF16, "Qt%d" % i) for i in range(2)]
    Qt_b = [Buf("Qt%d" % i) for i in range(2)]
    Kt = [A.alloc([128, 8, GT], BF16, "Kt%d" % i) for i in range(2)]
    Kt_b = [Buf("Kt%d" % i) for i in range(2)]
    LFt = [A.alloc([128, 8, GT], F32, "LFt%d" % i) for i in range(2)]
    LFt_b = [Buf("LFt%d" % i) for i in range(2)]
    Vt = [A.alloc([128, 2, D], BF16, "Vt%d" % i) for i in range(2)]
    Vt_b = [Buf("Vt%d" % i) for i in range(2)]
    QS = A.alloc([128, 8, GT], BF16, "QS")
    KS = A.alloc([128, 8, GT], BF16, "KS")
    QO = A.alloc([128, 8, GT], BF16, "QO")
    KST = A.alloc([128, 8, GT], BF16, "KST")
    prep_b = [{n_: Buf(n_) for n_ in ("QS", "KS", "QO", "KST", "DEC")} for h in range(8)]
    DEC = A.alloc([128, 8, 8], F32, "DEC")
    NTMP = 2
    tmpn = ("Bc", "CU", "D1", "E1", "E2", "EQ", "D2", "EK")
    TMP = [{n_: A.alloc([128, GT], F32, n_) for n_ in tmpn} for _ in range(NTMP)]
    TMP_b = [{n_: Buf(n_) for n_ in tmpn} for _ in range(NTMP)]
    KTK = A.alloc([128, 8, 2, 128], BF16, "KTK")
    KTK_b = [[Buf("KTK") for g in range(2)] for h in range(8)]
    KTM = [A.alloc([128, 4, 128], BF16, "KTM%d" % i) for i in range(4)]
    KTM_b = [Buf("KTM%d" % i) for i in range(4)]
    PT = [A.alloc([128, 4, 128], BF16, "PT%d" % i) for i in range(2)]
    PT_b = [Buf("PT%d" % i) for i in range(2)]
    O = A.alloc([128, 8, GT], F32, "O")
    O_b = Buf("O")
    m2 = A.mark()

    for d_ in range(2):
        A.reset(m2)
        last = d_ == 1
        if last:
            OFt = [A.alloc([128, 8, GT], F32, "OFt%d" % i) for i in range(2)]
            OFt_b = [Buf("OFt%d" % i) for i in range(2)]
            Gt = [A.alloc([128, 8, GT], BF16, "Gt%d" % i) for i in range(2)]
            Gt_b = [Buf("Gt%d" % i) for i in range(2)]
            H = [A.alloc([128, 8, GT], F32, "H%d" % i) for i in range(2)]
            H_b = [Buf("H%d" % i) for i in range(2)]
            W = NormBufs(k, GT)
            ON = A.alloc([128, 8, GT], BF16, "ON")
            ON_b = Buf("ON")
            Fo = A.alloc([128, 8, GT], F32, "Fo")
            Fo_b = Buf("Fo")
            RH = A.alloc([128, GT], F32, "RH")
            RH_b = Buf("RH")
        order = []
        for b in range(NB):
            order.append((b, 0))
            lat = list(range(1, NT))
            if d_ == 1:
                lat = lat[::-1]
            order += [(b, t) for t in lat]
        order = order[:k.cfg.get("maxtiles", 1000)]
        mask = cmask[:, d_, :]

        def loads(i):
            b, t = order[i]
            s = i % 2
            g0 = t * GT
            P.dma(Qt[s], fm_ap(Qs[b], g0, GT), [sQ[b]], [Qt_b[s]], key=Qt_b[s])
            P.dma(Kt[s], fm_ap(Kd[b][d_], g0, GT), [sK[b][d_]], [Kt_b[s]], key=Kt_b[s])
            P.dma(LFt[s], fm_ap(LFd[b][d_], g0, GT), [sLF[b][d_]], [LFt_b[s]], key=LFt_b[s])
            P.dma(Vt[s], Vs[b][g0:g0 + GT, :].rearrange("(j p) d -> p j d", p=128), [sV[b]], [Vt_b[s]], key=Vt_b[s])
            if last:
                isctx, t0 = tile_src(b, t)
                P.dma(OFt[s], fm_ap(Of[b], g0, GT), [sOf[b]], [OFt_b[s]], key=OFt_b[s])
                P.dma(Gt[s], fm_ap(Gs[b], g0, GT), [sG[b]], [Gt_b[s]], key=Gt_b[s])
                P.dma(H[s], h_ap(k, isctx, b, t0, GT), [h_buf(k, isctx, b)], [H_b[s]], key=H_b[s])

        loads(0)
        pcc = 0
        ktm_i = 0
        for i, (b, t) in enumerate(order):
            s = i % 2
            g0 = t * GT
            isctx, t0 = tile_src(b, t)
            row = 2 if isctx else b
            if i + 1 < len(order):
                loads(i + 1)
            if t == 0:
                for h in range(8):
                    P.add("dve", "memset", (S32[:, h, :], 0.0), None, [], [S_b[h]])
                    P.add("pool", "memset", (Sbf[:, h, :], 0.0), None, [], [Sbf_b[h]])
            for h in range(8):
                T_, Tb = TMP[h % NTMP], TMP_b[h % NTMP]
                lf = LFt[s][:, h, :]
                q = Qt[s][:, h, :]
                kk_ = Kt[s][:, h, :]
                P.add("dve", "tensor_tensor_scan", (T_["Bc"], rmask, lf, 0.0, ALU.mult, ALU.add), None,
                      [rmask_b, LFt_b[s]], [Tb["Bc"]])
                bc3 = T_["Bc"].rearrange("p (c j) -> p c j", j=32)
                tot = bc3[:, :, 31:32]
                if d_ == 0:
                    cum, cum_b = T_["Bc"], Tb["Bc"]
                else:
                    cum, cum_b = T_["CU"], Tb["CU"]
                    P.add("dve", "tensor_tensor", (), dict(out=cum, in0=lf, in1=T_["Bc"], op=ALU.subtract),
                          [LFt_b[s], Tb["Bc"]], [cum_b])
                    c3_ = cum.rearrange("p (c j) -> p c j", j=32)
                    P.add("dve", "tensor_tensor", (), dict(out=c3_, in0=c3_, in1=tot.broadcast_to([128, 8, 32]),
                          op=ALU.add), [cum_b, Tb["Bc"]], [cum_b])
                c3 = cum.rearrange("p (c j) -> p c j", j=32)
                mid = c3[:, :, 15:16]
                d1_3 = T_["D1"].rearrange("p (c j) -> p c j", j=32)
                P.add("dve", "tensor_tensor", (), dict(out=d1_3, in0=c3, in1=mid.broadcast_to([128, 8, 32]),
                      op=ALU.subtract), [cum_b], [Tb["D1"]])
                P.act(T_["E1"], T_["D1"], AF.Exp, [Tb["D1"]], [Tb["E1"]])
                P.act(T_["E2"], T_["D1"], AF.Exp, [Tb["D1"]], [Tb["E2"]], scale=-1.0)
                P.act(T_["EQ"], cum, AF.Exp, [cum_b], [Tb["EQ"]])
                d2_3 = T_["D2"].rearrange("p (c j) -> p c j", j=32)
                P.add("dve", "tensor_tensor", (), dict(out=d2_3, in0=c3, in1=tot.broadcast_to([128, 8, 32]),
                      op=ALU.subtract), [cum_b, Tb["Bc"]], [Tb["D2"]])
                P.act(T_["EK"], T_["D2"], AF.Exp, [Tb["D2"]], [Tb["EK"]], scale=-1.0)
                P.act(DEC[:, h, :], bc3[:, :, 31], AF.Exp, [Tb["Bc"]], [prep_b[h]["DEC"]])
                P.add("pool", "tensor_tensor", (), dict(out=QS[:, h, :], in0=q, in1=T_["E1"], op=ALU.mult),
                      [Qt_b[s], Tb["E1"]], [prep_b[h]["QS"]])
                P.add("pool", "tensor_tensor", (), dict(out=KS[:, h, :], in0=kk_, in1=T_["E2"], op=ALU.mult),
                      [Kt_b[s], Tb["E2"]], [prep_b[h]["KS"]])
                P.add("pool", "tensor_tensor", (), dict(out=QO[:, h, :], in0=q, in1=T_["EQ"], op=ALU.mult),
                      [Qt_b[s], Tb["EQ"]], [prep_b[h]["QO"]])
                P.add("dve", "tensor_tensor", (), dict(out=KST[:, h, :], in0=kk_, in1=T_["EK"], op=ALU.mult),
                      [Kt_b[s], Tb["EK"]], [prep_b[h]["KST"]])
                pi = 6
                for g in range(2):
                    trp = k.ps[pi].bitcast(BF16)[:, g * 128:(g + 1) * 128]
                    P.tr(trp, KST[:, h, g * 128:(g + 1) * 128], k.identb, [prep_b[h]["KST"], k.identb_b], [k.psb[pi]])
                P.act(KTK[:, h, :, :], k.ps[pi].bitcast(BF16)[:, 0:256].rearrange("p (g t) -> p g t", g=2), AF.Copy,
                      [k.psb[pi]], [KTK_b[h][0], KTK_b[h][1]])
            for g in ((0, 1) if d_ == 0 else (1, 0)):
                gs_ = slice(g * 128, (g + 1) * 128)
                for hs in range(2):
                    heads = range(hs * 4, hs * 4 + 4)
                    pA = pcc % 2
                    pcc += 1
                    pt, ptb = PT[pA], PT_b[pA]
                    for hi, h in enumerate(heads):
                        P.mm(k.ps[pA][:, hi * 128:(hi + 1) * 128], KS[:, h, gs_], QS[:, h, gs_], True, True,
                             [prep_b[h]["KS"], prep_b[h]["QS"]], [k.psb[pA]])
                    for hi, h in enumerate(heads):
                        P.add("dve", "tensor_tensor", (), dict(out=pt[:, hi, :], in0=k.ps[pA][:, hi * 128:(hi + 1) * 128],
                              in1=mask, op=ALU.mult), [k.psb[pA], cmask_b], [ptb])
                    for hi, h in enumerate(heads):
                        P.mm(k.ps[2][:, hi * 128:(hi + 1) * 128], Vt[s][:, g, h * 128:(h + 1) * 128], pt[:, hi, :], True, True,
                             [Vt_b[s], ptb], [k.psb[2]])
                    ktm = {}
                    for hi, h in enumerate(heads):
                        km, kmb = KTM[ktm_i % 4], KTM_b[ktm_i % 4]
                        ktm_i += 1
                        for c in range(4):
                            P.add("pool", "tensor_scalar", (), dict(out=km[:, c, :], in0=KTK[:, h, g, :],
                                  scalar1=rowm[:, c:c + 1], scalar2=1.0, op0=ALU.mult, op1=ALU.mult), [KTK_b[h][g], rowm_b], [kmb])
                        ktm[h] = (km, kmb)
                    for c in ((0, 1, 2, 3) if d_ == 0 else (3, 2, 1, 0)):
                        pD = 4 + (pcc % 2)
                        pcc += 1
                        ch = g * 4 + c
                        for hi, h in enumerate(heads):
                            P.mm(k.ps[3][:, hi * 128 + c * 32:hi * 128 + (c + 1) * 32], Sbf[:, h, :],
                                 QO[:, h, ch * 32:(ch + 1) * 32], True, True, [Sbf_b[h], prep_b[h]["QO"]], [k.psb[3]])
                            km, kmb = ktm[h]
                            P.mm(k.ps[pD][:, hi * 128:(hi + 1) * 128], km[:, c, :], Vt[s][:, g, h * 128:(h + 1) * 128],
                                 True, True, [kmb, Vt_b[s]], [k.psb[pD]])
                        for hi, h in enumerate(heads):
                            P.add("dve", "scalar_tensor_tensor", (), dict(out=S32[:, h, :], in0=S32[:, h, :],
                                  scalar=DEC[:, h, ch:ch + 1], in1=k.ps[pD][:, hi * 128:(hi + 1) * 128], op0=ALU.mult,
                                  op1=ALU.add), [S_b[h], prep_b[h]["DEC"], k.psb[pD]], [S_b[h]])
                            P.act(Sbf[:, h, :], S32[:, h, :], AF.Copy, [S_b[h]], [Sbf_b[h]])
                    ov = O[:, hs * 4:hs * 4 + 4, gs_]
                    P.act(ov, k.ps[2].rearrange("p (h t) -> p h t", h=4), AF.Copy, [k.psb[2]], [O_b])
                    P.add("dve", "tensor_tensor", (), dict(out=ov, in0=k.ps[3].rearrange("p (h t) -> p h t", h=4), in1=ov,
                          op=ALU.add), [k.psb[3], O_b], [O_b])
            if not last:
                P.dma(fm_ap(Of[b], g0, GT), O, [O_b], [sOf[b]], key=O_b, eng="pool")
                continue
            if isctx and layer == 3:
                continue
            P.add("pool", "tensor_tensor", (), dict(out=O, in0=O, in1=OFt[s], op=ALU.add), [O_b, OFt_b[s]], [O_b])
            P.act(W.SQ, O, AF.Square, [O_b], [W.SQ_b])
            for h in range(8):
                pi = 5
                P.mm(k.ps[pi][:, 0:GT], k.ones_bf, W.SQ[:, h, :], True, True, [k.ones_b, W.SQ_b], [k.psb[pi]])
                P.act(RH, k.ps[pi][:, 0:GT], AF.Sqrt, [k.psb[pi]], [RH_b], bias=EPS, scale=1.0 / 128)
                P.add("dve", "reciprocal", (), dict(out=RH, in_=RH), [RH_b], [RH_b])
                T1, T1_b = W.T1[h % 2], W.T1_b[h % 2]
                P.add("dve", "scalar_tensor_tensor", (), dict(out=T1, in0=O[:, h, :], scalar=gno[:, h:h + 1], in1=RH,
                      op0=ALU.mult, op1=ALU.mult), [O_b, gno_b, RH_b], [T1_b])
                P.add("pool", "tensor_tensor", (), dict(out=ON[:, h, :], in0=T1, in1=Gt[s][:, h, :], op=ALU.mult),
                      [T1_b, Gt_b[s]], [ON_b])
            for c in range(8):
                pi = pcc % 2
                pcc += 1
                for kk in range(8):
                    P.mm(k.ps[pi][:, 0:GT], wo[:, kk, c * 128:(c + 1) * 128], ON[:, kk, :], kk == 0, kk == 7,
                         [wo_b, ON_b], [k.psb[pi]])
                evac(k, Fo[:, c, :], k.ps[pi][:, 0:GT], c, [k.psb[pi]], [Fo_b])
            postnorm_res(k, layer, 0, Fo, Fo_b, H[s], H_b[s], GT, row, W)
            P.dma(h_ap(k, isctx, b, t0, GT), H[s], [H_b[s]], [h_buf(k, isctx, b)], key=H_b[s], eng="pool")
        P.barrier()
    A.reset(m0)
    P.barrier()


def stage_s5(k, layer, j):
    P, A, nc = k.P, k.A, k.nc
    TC = 128
    LT = LC + L
    NCH = LT // TC
    PI = float(np.pi)
    m0 = A.mark()
    Ud = [nc.dram_tensor("s5_u%d" % b, [D, LT], BF16, kind="Internal").ap() for b in range(NB)]
    Yd = [nc.dram_tensor("s5_y%d" % b, [D, LT], F32, kind="Internal").ap() for b in range(NB)]
    sU = [Buf("s5U") for b in range(NB)]
    sY = [Buf("s5Y") for b in range(NB)]
    wg = A.alloc([128, 8, 2 * D], BF16, "wglu")
    wg_b = Buf("wglu")
    mst = A.mark()
    stg = [A.alloc([128, 1024], F32, "stg%d" % i) for i in range(2)]
    stg_b = [Buf("stg%d" % i) for i in range(2)]
    load_weight_bf16(k, wg, wg_b, k.din["s5_w_glu"], k.dbuf["s5_w_glu"], 8, 2 * D, stg, stg_b)
    P.barrier()
    A.reset(mst)
    dsk = A.alloc([128, 8], F32, "dskip")
    dsk_b = Buf("dskip")
    P.dma(dsk, k.din["s5_dT"], [k.dbuf["s5_dT"]], [dsk_b], key=dsk_b)
    BBm = A.alloc([128, 64, 128], BF16, "BBm")
    BBm_b = Buf("BBm")
    Cm = A.alloc([128, 64, 128], BF16, "Cm")
    Cm_b = Buf("Cm")
    A1 = A.alloc([128, 2, 32], F32, "A1")
    A2 = A.alloc([128, 2, 32], F32, "A2")
    A_b = Buf("Acoef")
    XH = A.alloc([128, TC + 1, 64], F32, "XH")
    XH_b = Buf("XH")
    BU = A.alloc([128, TC, 64], F32, "BU")
    BU_b = Buf("BU")
    XHb = BU.rearrange("p t c -> p (t c)").bitcast(BF16)[:, 0:TC * 64].rearrange("p (t c) -> p t c", c=64)
    XHb_b = BU_b
    T0 = A.alloc([128, 2, 32], F32, "T0")
    T1 = A.alloc([128, 2, 32], F32, "T1")
    T0_b, T1_b = Buf("T0"), Buf("T1")
    U = [A.alloc([128, 8, TC], BF16, "U%d" % i) for i in range(2)]
    U_b = [Buf("U%d" % i) for i in range(2)]
    H = [A.alloc([128, 8, TC], F32, "H%d" % i) for i in range(2)]
    H_b = [Buf("H%d" % i) for i in range(2)]
    Yt = [A.alloc([128, 8, TC], F32, "Yt%d" % i) for i in range(2)]
    Yt_b = [Buf("Yt%d" % i) for i in range(2)]
    W = NormBufs(k, TC)
    m1 = A.mark()

    def chunk_src(c):
        if c < LC // TC:
            return True, c * TC
        return False, (c - LC // TC) * TC

    for d_ in range(2):
        A.reset(m1)
        sm = A.mark()
        LR = A.alloc([128, 32], F32); LI = A.alloc([128, 32], F32); DT = A.alloc([128, 32], F32)
        BR = A.alloc([128, 32, 16], F32); BI = A.alloc([128, 32, 16], F32)
        CR = A.alloc([128, 32, 16], F32); CI = A.alloc([128, 32, 16], F32)
        raw_b = Buf("s5raw")
        for dst, nm in ((LR, "s5_LR"), (LI, "s5_LI"), (DT, "s5_DT"), (CR, "s5_CRE"), (CI, "s5_CIM")):
            P.dma(dst, k.din[nm][d_], [k.dbuf[nm]], [raw_b], key=raw_b)
        for dst, nm in ((BR, "s5_BRE"), (BI, "s5_BIM")):
            P.dma(dst, k.din[nm], [k.dbuf[nm]], [raw_b], key=raw_b)
        names = ("dt", "lrdt", "mag", "ang", "kk", "tmp", "sn", "cs", "are", "aim", "den", "fre", "fim", "am1", "t2")
        V = {n_: A.alloc([128, 32], F32, n_) for n_ in names}
        cb = Buf("s5coef")

        def dv(meth, **kw):
            P.add("dve", meth, (), kw, [raw_b, cb], [cb])

        dv("tensor_scalar", out=LR, in0=LR, scalar1=-1e-4, scalar2=1.0, op0=ALU.min, op1=ALU.mult)
        P.act(V["dt"], DT, AF.Exp, [raw_b, cb], [cb])
        dv("tensor_tensor", out=V["lrdt"], in0=LR, in1=V["dt"], op=ALU.mult)
        P.act(V["mag"], V["lrdt"], AF.Exp, [cb], [cb])
        dv("tensor_tensor", out=V["ang"], in0=LI, in1=V["dt"], op=ALU.mult)

        def sin_of(dst, shift):
            dv("tensor_scalar", out=V["tmp"], in0=V["ang"], scalar1=shift, scalar2=1.0, op0=ALU.add, op1=ALU.mult)
            dv("tensor_scalar", out=V["kk"], in0=V["tmp"], scalar1=PI, scalar2=1.0, op0=ALU.is_gt, op1=ALU.mult)
            for m_ in (3, 5, 7):
                dv("tensor_scalar", out=V["t2"], in0=V["tmp"], scalar1=m_ * PI, scalar2=1.0, op0=ALU.is_gt, op1=ALU.mult)
                dv("tensor_tensor", out=V["kk"], in0=V["kk"], in1=V["t2"], op=ALU.add)
            dv("scalar_tensor_tensor", out=V["tmp"], in0=V["kk"], scalar=-2.0 * PI, in1=V["tmp"], op0=ALU.mult, op1=ALU.add)
            P.act(dst, V["tmp"], AF.Sin, [cb], [cb])

        sin_of(V["sn"], 0.0)
        sin_of(V["cs"], PI / 2)
        dv("tensor_tensor", out=V["are"], in0=V["mag"], in1=V["cs"], op=ALU.mult)
        dv("tensor_tensor", out=V["aim"], in0=V["mag"], in1=V["sn"], op=ALU.mult)
        P.add("dve", "tensor_copy", (A1[:, 0, :], V["are"]), None, [cb], [A_b])
        P.add("dve", "tensor_copy", (A1[:, 1, :], V["are"]), None, [cb], [A_b])
        P.add("dve", "tensor_scalar", (), dict(out=A2[:, 0, :], in0=V["aim"], scalar1=-1.0, scalar2=1.0, op0=ALU.mult,
              op1=ALU.mult), [cb], [A_b])
        P.add("dve", "tensor_copy", (A2[:, 1, :], V["aim"]), None, [cb], [A_b])
        dv("tensor_tensor", out=V["den"], in0=LR, in1=LR, op=ALU.mult)
        dv("tensor_tensor", out=V["t2"], in0=LI, in1=LI, op=ALU.mult)
        dv("tensor_tensor", out=V["den"], in0=V["den"], in1=V["t2"], op=ALU.add)
        dv("reciprocal", out=V["den"], in_=V["den"])
        dv("tensor_scalar", out=V["am1"], in0=V["are"], scalar1=-1.0, scalar2=1.0, op0=ALU.add, op1=ALU.mult)
        dv("tensor_tensor", out=V["fre"], in0=V["am1"], in1=LR, op=ALU.mult)
        dv("tensor_tensor", out=V["t2"], in0=V["aim"], in1=LI, op=ALU.mult)
        dv("tensor_tensor", out=V["fre"], in0=V["fre"], in1=V["t2"], op=ALU.add)
        dv("tensor_tensor", out=V["fre"], in0=V["fre"], in1=V["den"], op=ALU.mult)
        dv("tensor_tensor", out=V["fim"], in0=V["aim"], in1=LR, op=ALU.mult)
        dv("tensor_tensor", out=V["t2"], in0=V["am1"], in1=LI, op=ALU.mult)
        dv("tensor_tensor", out=V["fim"], in0=V["fim"], in1=V["t2"], op=ALU.subtract)
        dv("tensor_tensor", out=V["fim"], in0=V["fim"], in1=V["den"], op=ALU.mult)
        BBR = A.alloc([128, 32, 16], F32); BBI = A.alloc([128, 32, 16], F32); TB = A.alloc([128, 32, 16], F32)
        fre3 = V["fre"].unsqueeze(2).broadcast_to([128, 32, 16])
        fim3 = V["fim"].unsqueeze(2).broadcast_to([128, 32, 16])
        dv("tensor_tensor", out=BBR, in0=BR, in1=fre3, op=ALU.mult)
        dv("tensor_tensor", out=TB, in0=BI, in1=fim3, op=ALU.mult)
        dv("tensor_tensor", out=BBR, in0=BBR, in1=TB, op=ALU.subtract)
        dv("tensor_tensor", out=BBI, in0=BI, in1=fre3, op=ALU.mult)
        dv("tensor_tensor", out=TB, in0=BR, in1=fim3, op=ALU.mult)
        dv("tensor_tensor", out=BBI, in0=BBI, in1=TB, op=ALU.add)
        SRC = [A.alloc([128, 128], F32, "SRC%d" % i) for i in range(2)]
        SRC_b = [Buf("SRC%d" % i) for i in range(2)]
        P.add("pool", "memset", (Cm, 0.0), None, [], [Cm_b])
        it = 0
        for sc in range(32):
            scl = sc % 4
            for ri, srcm in ((0, BBR), (1, BBI)):
                sr, srb = SRC[it % 2], SRC_b[it % 2]
                it += 1
                P.add("pool", "memset", (sr, 0.0), None, [], [srb])
                P.add("pool", "tensor_copy", (sr[0:64, 32 * scl:32 * scl + 16], srcm[0:64, sc, :]), None, [cb], [srb])
                P.add("pool", "tensor_copy", (sr[64:128, 32 * scl + 16:32 * scl + 32], srcm[64:128, sc, :]), None, [cb], [srb])
                pi = it % 4
                P.tr(k.ps[pi][:, 0:128], sr, k.ident, [srb, k.ident_b], [k.psb[pi]])
                P.act(BBm[:, sc * 2 + ri, :], k.ps[pi][:, 0:128], AF.Copy, [k.psb[pi]], [BBm_b])
            for ri, srcm, sgn in ((0, CR, 1.0), (1, CI, -1.0)):
                P.add("dve", "tensor_scalar", (), dict(out=Cm[0:64, sc * 2 + ri, 32 * scl:32 * scl + 16], in0=srcm[0:64, sc, :],
                      scalar1=sgn, scalar2=1.0, op0=ALU.mult, op1=ALU.mult), [raw_b, Cm_b], [Cm_b])
                P.add("dve", "tensor_scalar", (), dict(out=Cm[64:128, sc * 2 + ri, 32 * scl + 16:32 * scl + 32],
                      in0=srcm[64:128, sc, :], scalar1=sgn, scalar2=1.0, op0=ALU.mult, op1=ALU.mult), [raw_b, Cm_b], [Cm_b])
        P.barrier()
        A.reset(sm)
        if d_ == 1:
            Yf = [A.alloc([128, 8, TC], F32, "Yf%d" % i) for i in range(2)]
            Yf_b = [Buf("Yf%d" % i) for i in range(2)]
            Zb = A.alloc([128, 8, TC], BF16, "Zb")
            Zb_b = Buf("Zb")
            Fo = A.alloc([128, 8, TC], F32, "Fo")
            Fo_b = Buf("Fo")
            G1 = A.alloc([128, 8, TC], F32, "G1")
            G2 = A.alloc([128, 8, TC], F32, "G2")
            G_b = Buf("G")
            SGt = [A.alloc([128, TC], F32, "SGt%d" % i) for i in range(2)]
            SGt_b = [Buf("SGt%d" % i) for i in range(2)]
        order = []
        for b in range(NB):
            cc = list(range(LC // TC)); ll = list(range(LC // TC, NCH))
            if d_ == 1:
                cc, ll = cc[::-1], ll[::-1]
            order += [(b, c) for c in cc + ll]
        order = order[:k.cfg.get("maxtiles", 10000)]

        def loads(i):
            b, c = order[i]
            s = i % 2
            isctx, t0 = chunk_src(c)
            P.dma(H[s], h_ap(k, isctx, b, t0, TC), [h_buf(k, isctx, b)], [H_b[s]], key=H_b[s])
            if d_ == 1:
                P.dma(U[s], fm_ap(Ud[b], c * TC, TC), [sU[b]], [U_b[s]], key=U_b[s])
                P.dma(Yf[s], fm_ap(Yd[b], c * TC, TC), [sY[b]], [Yf_b[s]], key=Yf_b[s])

        loads(0)
        pcc = 0
        for i, (b, c) in enumerate(order):
            s = i % 2
            isctx, t0 = chunk_src(c)
            row = 2 if isctx else b
            if i + 1 < len(order):
                loads(i + 1)
            first = (c == (0 if d_ == 0 else LC // TC - 1))
            cin, cout = (0, TC) if d_ == 0 else (TC, 0)
            if first:
                P.add("dve", "memset", (XH[:, cin, :], 0.0), None, [], [XH_b])
            else:
                P.add("dve", "tensor_copy", (XH[:, cin, :], XH[:, cout, :]), None, [XH_b], [XH_b])
            if d_ == 0:
                prenorm_mod(k, layer, 0, H[s], H_b[s], TC, row, W, U[s], U_b[s])
                P.dma(fm_ap(Ud[b], c * TC, TC), U[s], [U_b[s]], [sU[b]], key=U_b[s], eng="pool")
            for ri in range(2):
                for sc0 in range(0, 32, 4):
                    pi = pcc % 4
                    pcc += 1
                    for q_ in range(4):
                        sc = sc0 + q_
                        P.mm(k.ps[pi][:, q_ * TC:(q_ + 1) * TC], BBm[:, sc * 2 + ri, :], U[s][:, sc // 4, :], True, True,
                             [BBm_b, U_b[s]], [k.psb[pi]])
                    c0 = ri * 32 + sc0
                    P.act(BU[:, :, c0:c0 + 4].rearrange("p t c -> p c t"),
                          k.ps[pi].rearrange("p (c t) -> p c t", c=4), AF.Copy, [k.psb[pi]], [BU_b])
            for jj in range(TC):
                if d_ == 0:
                    prev, cur, tok = jj, jj + 1, jj
                else:
                    tok = TC - 1 - jj
                    prev, cur = tok + 1, tok
                Pv = XH[:, prev, :].rearrange("p (r s) -> p r s", r=2)
                Cv = XH[:, cur, :].rearrange("p (r s) -> p r s", r=2)
                P.add("dve", "tensor_tensor", (), dict(out=T0, in0=Pv, in1=A1, op=ALU.mult), [XH_b, A_b], [T0_b])
                P.add("dve", "tensor_tensor", (), dict(out=T1[:, 0, :], in0=Pv[:, 1, :], in1=A2[:, 0, :], op=ALU.mult),
                      [XH_b, A_b], [T1_b])
                P.add("dve", "tensor_tensor", (), dict(out=T1[:, 1, :], in0=Pv[:, 0, :], in1=A2[:, 1, :], op=ALU.mult),
                      [XH_b, A_b], [T1_b])
                P.add("dve", "tensor_tensor", (), dict(out=T0, in0=T0, in1=T1, op=ALU.add), [T0_b, T1_b], [T0_b])
                P.add("dve", "tensor_tensor", (), dict(out=Cv, in0=T0, in1=BU[:, tok, :].rearrange("p (r s) -> p r s", r=2),
                      op=ALU.add), [T0_b, BU_b], [XH_b])
            off = 1 if d_ == 0 else 0
            P.add("pool", "tensor_copy", (XHb, XH[:, off:off + TC, :]), None, [XH_b], [XHb_b])
            for ci in range(8):
                pi = 4 + (pcc % 2)
                pcc += 1
                n_ = 0
                for scl in range(4):
                    sc = ci * 4 + scl
                    for ri in range(2):
                        P.mm(k.ps[pi][:, 0:TC], Cm[:, sc * 2 + ri, :], XHb[:, :, ri * 32 + sc], n_ == 0, n_ == 7,
                             [Cm_b, XHb_b], [k.psb[pi]])
                        n_ += 1
                if d_ == 0:
                    evac(k, Yt[s][:, ci, :], k.ps[pi][:, 0:TC], ci, [k.psb[pi]], [Yt_b[s]])
                else:
                    P.add("dve", "tensor_tensor", (), dict(out=Yt[s][:, ci, :], in0=k.ps[pi][:, 0:TC], in1=Yf[s][:, ci, :],
                          op=ALU.add), [k.psb[pi], Yf_b[s]], [Yt_b[s]])
            if d_ == 0:
                P.dma(fm_ap(Yd[b], c * TC, TC), Yt[s], [Yt_b[s]], [sY[b]], key=Yt_b[s], eng="pool")
                continue
            Y = Yt[s]
            for ci in range(8):
                P.add("dve", "scalar_tensor_tensor", (), dict(out=Y[:, ci, :], in0=U[s][:, ci, :], scalar=dsk[:, ci:ci + 1],
                      in1=Y[:, ci, :], op0=ALU.mult, op1=ALU.add), [U_b[s], dsk_b, Yt_b[s]], [Yt_b[s]])
            P.add("pool", "tensor_tensor", (), dict(out=G1, in0=Y, in1=Y, op=ALU.mult), [Yt_b[s]], [G_b])
            P.add("pool", "tensor_scalar", (), dict(out=G1, in0=G1, scalar1=0.044715, scalar2=1.0, op0=ALU.mult, op1=ALU.add),
                  [G_b], [G_b])
            P.add("pool", "tensor_tensor", (), dict(out=G1, in0=G1, in1=Y, op=ALU.mult), [G_b, Yt_b[s]], [G_b])
            P.act(G2, G1, AF.Sigmoid, [G_b], [G_b], scale=2.0 * 0.7978845608028654)
            P.add("pool", "tensor_tensor", (), dict(out=Zb, in0=G2, in1=Y, op=ALU.mult), [G_b, Yt_b[s]], [Zb_b])
            for cc_ in range(8):
                pv = pcc % 4
                pcc += 1
                for kk in range(8):
                    P.mm(k.ps[pv][:, 0:TC], wg[:, kk, cc_ * 128:(cc_ + 1) * 128], Zb[:, kk, :], kk == 0, kk == 7,
                         [wg_b, Zb_b], [k.psb[pv]])
                pg = pcc % 4
                pcc += 1
                for kk in range(8):
                    P.mm(k.ps[pg][:, 0:TC], wg[:, kk, D + cc_ * 128:D + (cc_ + 1) * 128], Zb[:, kk, :], kk == 0, kk == 7,
                         [wg_b, Zb_b], [k.psb[pg]])
                sg, sgb = SGt[cc_ % 2], SGt_b[cc_ % 2]
                P.act(sg, k.ps[pg][:, 0:TC], AF.Sigmoid, [k.psb[pg]], [sgb])
                P.add("dve", "tensor_tensor", (), dict(out=Fo[:, cc_, :], in0=k.ps[pv][:, 0:TC], in1=sg, op=ALU.mult),
                      [k.psb[pv], sgb], [Fo_b])
            postnorm_res(k, layer, 0, Fo, Fo_b, H[s], H_b[s], TC, row, W)
            P.dma(h_ap(k, isctx, b, t0, TC), H[s], [H_b[s]], [h_buf(k, isctx, b)], key=H_b[s], eng="pool")
        P.barrier()
    A.reset(m0)
    P.barrier()


def stage_na(k, layer, j):
    P, A, nc = k.P, k.A, k.nc
    m0 = A.mark()
    Qn = [nc.dram_tensor("na_q%d" % b, [D, L], BF16, kind="Internal").ap() for b in range(NB)]
    Kn = [nc.dram_tensor("na_k%d" % b, [D, L], BF16, kind="Internal").ap() for b in range(NB)]
    Kc = [nc.dram_tensor("na_kc%d" % b, [D, LC], BF16, kind="Internal").ap() for b in range(NB)]
    Vn = [nc.dram_tensor("na_v%d" % b, [L, D], BF16, kind="Internal").ap() for b in range(NB)]
    Vc = [nc.dram_tensor("na_vc%d" % b, [LC, D], BF16, kind="Internal").ap() for b in range(NB)]
    sQ = [Buf("naQ") for b in range(NB)]
    sK = [Buf("naK") for b in range(NB)]
    sKc = [Buf("naKc") for b in range(NB)]
    sV = [Buf("naV") for b in range(NB)]
    sVc = [Buf("naVc") for b in range(NB)]
    wo = A.alloc([128, 8, D], BF16, "nwo")
    wo_b = Buf("nwo")
    m1 = A.mark()
    w1 = A.alloc([128, 8, 3 * D], BF16, "nw1")
    w1_b = Buf("nw1")
    stg = [A.alloc([128, 1024], F32, "stg%d" % i) for i in range(2)]
    stg_b = [Buf("stg%d" % i) for i in range(2)]
    load_weight_bf16(k, w1, w1_b, k.din["na_w_qkv"], k.dbuf["na_w_qkv"], 8, 3 * D, stg, stg_b)
    load_weight_bf16(k, wo, wo_b, k.din["na_w_out"], k.dbuf["na_w_out"], 8, D, stg, stg_b)
    H = [A.alloc([128, 8, TT], F32, "H%d" % i) for i in range(2)]
    H_b = [Buf("H%d" % i) for i in range(2)]
    W = NormBufs(k, TT)
    Aa = A.alloc([128, 8, TT], BF16, "Aa")
    Aa_b = Buf("Aa")
    Qt = A.alloc([128, 8, TT], BF16, "Qt")
    Qt_b = Buf("Qt")
    Kt = A.alloc([128, 8, TT], BF16, "Kt")
    Kt_b = Buf("Kt")
    Vt = A.alloc([128, 4, D], BF16, "Vt")
    Vt_b = Buf("Vt")
    tiles = seq_tiles()[:k.cfg.get("maxtiles", 1000)]

    def load1(i):
        isctx, b, t, t0, n = tiles[i]
        P.dma(H[i % 2][:, :, 0:n], h_ap(k, isctx, b, t0, n), [h_buf(k, isctx, b)], [H_b[i % 2]], key=H_b[i % 2])

    load1(0)
    pc = 0
    for i, (isctx, b, t, t0, n) in enumerate(tiles):
        s = i % 2
        row = 2 if isctx else b
        if i + 1 < len(tiles):
            load1(i + 1)
        prenorm_mod(k, layer, 0, H[s], H_b[s], n, row, W, Aa, Aa_b)
        for c in range(8):
            if not isctx:
                pi = pc % 6
                pc += 1
                for kk in range(8):
                    P.mm(k.ps[pi][:, 0:n], w1[:, kk, c * 128:(c + 1) * 128], Aa[:, kk, 0:n], kk == 0, kk == 7,
                         [w1_b, Aa_b], [k.psb[pi]])
                P.act(Qt[:, c, 0:n], k.ps[pi][:, 0:n], AF.Identity, [k.psb[pi]], [Qt_b], scale=0.125)
            pi = pc % 6
            pc += 1
            for kk in range(8):
                P.mm(k.ps[pi][:, 0:n], w1[:, kk, D + c * 128:D + (c + 1) * 128], Aa[:, kk, 0:n], kk == 0, kk == 7,
                     [w1_b, Aa_b], [k.psb[pi]])
            P.add("dve", "tensor_copy", (Kt[:, c, 0:n], k.ps[pi][:, 0:n]), None, [k.psb[pi]], [Kt_b])
        for blk in range(n // 128):
            for half in range(2):
                pi = pc % 6
                pc += 1
                for kk in range(8):
                    P.mm(k.ps[pi], Aa[:, kk, blk * 128:(blk + 1) * 128],
                         w1[:, kk, 2 * D + half * 512:2 * D + (half + 1) * 512], kk == 0, kk == 7, [w1_b, Aa_b], [k.psb[pi]])
                evac(k, Vt[:, blk, half * 512:(half + 1) * 512], k.ps[pi], blk * 2 + half, [k.psb[pi]], [Vt_b])
        nb_ = n // 128
        if isctx:
            P.dma(fm_ap(Kc[b], t0, n), Kt[:, :, 0:n], [Kt_b], [sKc[b]], key=Kt_b, eng="pool")
            P.dma(Vc[b][t0:t0 + n, :].rearrange("(j p) d -> p j d", p=128), Vt[:, 0:nb_, :], [Vt_b], [sVc[b]], key=Vt_b, eng="pool")
        else:
            P.dma(fm_ap(Qn[b], t0, n), Qt[:, :, 0:n], [Qt_b], [sQ[b]], key=Qt_b, eng="pool")
            P.dma(fm_ap(Kn[b], t0, n), Kt[:, :, 0:n], [Kt_b], [sK[b]], key=Kt_b, eng="pool")
            P.dma(Vn[b][t0:t0 + n, :].rearrange("(j p) d -> p j d", p=128), Vt[:, 0:nb_, :], [Vt_b], [sV[b]], key=Vt_b, eng="pool")
    P.barrier()
    A.reset(m1)
    if k.cfg.get("p1only"):
        A.reset(m0)
        return
    VAL = A.alloc([128, 24, TT], BF16, "VAL")
    VAL_b = Buf("VAL")
    stg = [A.alloc([128, TT], F32, "vstg%d" % i) for i in range(2)]
    stg_b = [Buf("vstg%d" % i) for i in range(2)]
    for vi in range(24):
        sg_ = vi % 2
        P.dma(stg[sg_], k.din["na_valid"][vi // 8, vi % 8], [k.dbuf["na_valid"]], [stg_b[sg_]], key=stg_b[sg_])
        P.add("pool", "tensor_copy", (VAL[:, vi, :], stg[sg_]), None, [stg_b[sg_]], [VAL_b])
    Qb = A.alloc([128, 8, TT], BF16, "Qb")
    Qb_b = Buf("Qb")
    Kw = A.alloc([128, 8, 2 * TT], BF16, "Kw")
    Kw_b = Buf("Kw")
    Vw = A.alloc([128, 8, D], BF16, "Vw")
    Vw_b = Buf("Vw")
    Kcx = A.alloc([128, 8, LC], BF16, "Kcx")
    Kcx_b = Buf("Kcx")
    Vcx = A.alloc([128, 2, D], BF16, "Vcx")
    Vcx_b = Buf("Vcx")
    Hn = A.alloc([128, 8, TT], F32, "Hn")
    Hn_b = Buf("Hn")
    W = NormBufs(k, TT)
    BT = [A.alloc([128, TT], F32, "BT%d" % i) for i in range(4)]
    BT_b = [Buf("BT%d" % i) for i in range(4)]
    Et = [A.alloc([128, TT], BF16, "Et%d" % i) for i in range(3)]
    Et_b = [Buf("Et%d" % i) for i in range(3)]
    Pm = [A.alloc([128, TT], BF16, "Pm%d" % i) for i in range(3)]
    Pm_b = [Buf("Pm%d" % i) for i in range(3)]
    RZ = [A.alloc([128, TT], F32, "RZ%d" % i) for i in range(2)]
    RZ_b = [Buf("RZ%d" % i) for i in range(2)]
    ON = A.alloc([128, 8, TT], BF16, "ON")
    ON_b = Buf("ON")
    Fo = A.alloc([128, 8, TT], F32, "Fo")
    Fo_b = Buf("Fo")
    tbl = k.din["na_tbl"]
    bands = [(b, bd) for b in range(NB) for bd in range(8)][:k.cfg.get("maxbands", 1000)]
    bt_i = 0
    e_i = 0
    ps_i = 0
    for (b, bd) in bands:
        variant = 0 if bd == 0 else (2 if bd == 7 else 1)
        r0 = min(max(8 * bd - 4, 0), 48)
        delta = r0 - 8 * bd
        t0 = bd * TT
        if bd == 0:
            P.dma(Kcx, fm_ap(Kc[b], 0, LC), [sKc[b]], [Kcx_b], key=Kcx_b)
            P.dma(Vcx, Vc[b].rearrange("(j p) d -> p j d", p=128), [sVc[b]], [Vcx_b], key=Vcx_b)
        P.dma(Qb, fm_ap(Qn[b], t0, TT), [sQ[b]], [Qb_b], key=Qb_b)
        P.dma(Kw, fm_ap(Kn[b], 64 * r0, 2 * TT), [sK[b]], [Kw_b], key=Kw_b)
        P.dma(Vw, Vn[b][64 * r0:64 * r0 + 2 * TT, :].rearrange("(j p) d -> p j d", p=128), [sV[b]], [Vw_b], key=Vw_b)
        P.dma(Hn, h_ap(k, False, b, t0, TT), [k.hT_buf[b]], [Hn_b], key=Hn_b)
        for h in range(k.cfg.get("maxheads", 16)):
            ci, pb = h // 2, 64 * (h % 2)
            pO, pZ = 3 + (h % 2), 5 + (h % 2)
            for kt in range(8):
                bt, btb = BT[bt_i % 4], BT_b[bt_i % 4]
                bt_i += 1
                for krl in range(2):
                    e0 = 15 - delta - 2 * kt - krl
                    P.dma(bt[krl * 64:(krl + 1) * 64, :].rearrange("p (e q) -> p e q", e=8),
                          tbl[h, e0:e0 + 8].rearrange("e kc q -> kc e q"), [k.dbuf["na_tbl"]], [btb], key=btb)
                pS = ps_i % 3
                ps_i += 1
                P.mm(k.ps[pS], Kw[pb:pb + 64, ci, kt * 128:(kt + 1) * 128], Qb[pb:pb + 64, ci, :], True, False,
                     [Kw_b, Qb_b], [k.psb[pS]])
                P.mm(k.ps[pS], k.ident, bt, False, True, [k.ident_b, btb], [k.psb[pS]])
                et, etb = Et[e_i % 3], Et_b[e_i % 3]
                pm, pmb = Pm[e_i % 3], Pm_b[e_i % 3]
                e_i += 1
                P.act(et, k.ps[pS], AF.Exp, [k.psb[pS]], [etb])
                P.add("pool" if kt % 2 == 0 else "dve", "tensor_tensor", (), dict(out=pm, in0=et,
                      in1=VAL[:, variant * 8 + kt, :], op=ALU.mult), [etb, VAL_b], [pmb])
                P.mm(k.ps[pO], Vw[:, kt, ci * 128:(ci + 1) * 128], pm, kt == 0, False, [Vw_b, pmb], [k.psb[pO]])
                P.mm(k.ps[pZ], k.ones_bf, pm, kt == 0, False, [k.ones_b, pmb], [k.psb[pZ]])
            for jx in range(2):
                pS = ps_i % 3
                ps_i += 1
                P.mm(k.ps[pS], Kcx[pb:pb + 64, ci, jx * 128:(jx + 1) * 128], Qb[pb:pb + 64, ci, :], True, True,
                     [Kcx_b, Qb_b], [k.psb[pS]])
                et, etb = Et[e_i % 3], Et_b[e_i % 3]
                e_i += 1
                P.act(et, k.ps[pS], AF.Exp, [k.psb[pS]], [etb])
                P.mm(k.ps[pO], Vcx[:, jx, ci * 128:(ci + 1) * 128], et, False, jx == 1, [Vcx_b, etb], [k.psb[pO]])
                P.mm(k.ps[pZ], k.ones_bf, et, False, jx == 1, [k.ones_b, etb], [k.psb[pZ]])
            rz, rzb = RZ[h % 2], RZ_b[h % 2]
            P.add("dve", "reciprocal", (), dict(out=rz[pb:pb + 64, :], in_=k.ps[pZ][pb:pb + 64, :]), [k.psb[pZ]], [rzb])
            P.add("dve", "tensor_tensor", (), dict(out=ON[pb:pb + 64, ci, :], in0=k.ps[pO][pb:pb + 64, :],
                  in1=rz[pb:pb + 64, :], op=ALU.mult), [k.psb[pO], rzb], [ON_b])
        for c in range(8):
            pi = 7 if c % 2 == 0 else (ps_i % 3)
            if c % 2 == 1:
                ps_i += 1
            for kk in range(8):
                P.mm(k.ps[pi], wo[:, kk, c * 128:(c + 1) * 128], ON[:, kk, :], kk == 0, kk == 7, [wo_b, ON_b], [k.psb[pi]])
            evac(k, Fo[:, c, :], k.ps[pi], c, [k.psb[pi]], [Fo_b])
        postnorm_res(k, layer, 0, Fo, Fo_b, Hn, Hn_b, TT, b, W, psi=7)
        P.dma(h_ap(k, False, b, t0, TT), Hn, [Hn_b], [k.hT_buf[b]], key=Hn_b, eng="pool")
    A.reset(m0)
    P.barrier()


def stage_mlp(k, layer):
    P, A = k.P, k.A
    MT = 256
    m0 = A.mark()
    w1 = A.alloc([128, 8, DFF], BF16, "w1")
    w1_b = Buf("w1")
    w2 = A.alloc([128, 32, D], BF16, "w2")
    w2_b = Buf("w2")
    stg = [A.alloc([128, 1024], F32, "stg%d" % i) for i in range(2)]
    stg_b = [Buf("stg%d" % i) for i in range(2)]
    load_weight_bf16(k, w1, w1_b, k.din["mlp_w_in"][layer], k.dbuf["mlp_w_in"], 8, DFF, stg, stg_b)
    load_weight_bf16(k, w2, w2_b, k.din["mlp_w_out"][layer], k.dbuf["mlp_w_out"], 32, D, stg, stg_b)
    H = [A.alloc([128, 8, MT], F32, "H%d" % i) for i in range(2)]
    H_b = [Buf("H%d" % i) for i in range(2)]
    W = NormBufs(k, MT)
    Aa = A.alloc([128, 8, MT], BF16, "Aa")
    Aa_b = Buf("Aa")
    HID = A.alloc([128, 32, MT], BF16, "HID")
    HID_b = Buf("HID")
    RL = [A.alloc([128, MT], F32, "RL%d" % i) for i in range(2)]
    RL_b = [Buf("RL%d" % i) for i in range(2)]
    Fo = A.alloc([128, 8, MT], F32, "Fo")
    Fo_b = Buf("Fo")
    tiles = [tl for tl in seq_tiles(MT) if not (tl[0] and layer == 3)]

    def load(i):
        isctx, b, t, t0, n = tiles[i]
        P.dma(H[i % 2][:, :, 0:n], h_ap(k, isctx, b, t0, n), [h_buf(k, isctx, b)], [H_b[i % 2]], key=H_b[i % 2])

    load(0)
    pc = 0
    for i, (isctx, b, t, t0, n) in enumerate(tiles):
        s = i % 2
        row = 2 if isctx else b
        Hs, Hsb = H[s], H_b[s]
        if i + 1 < len(tiles):
            load(i + 1)
        prenorm_mod(k, layer, 1, Hs, Hsb, n, row, W, Aa, Aa_b)
        for m in range(32):
            pi = pc % 6
            pc += 1
            for c in range(8):
                P.mm(k.ps[pi][:, 0:n], w1[:, c, m * 128:(m + 1) * 128], Aa[:, c, 0:n], c == 0, c == 7,
                     [w1_b, Aa_b], [k.psb[pi]])
            rs = m % 2
            P.act(RL[rs][:, 0:n], k.ps[pi][:, 0:n], AF.Relu, [k.psb[pi]], [RL_b[rs]])
            P.add("pool", "tensor_tensor", (), dict(out=HID[:, m, 0:n], in0=RL[rs][:, 0:n], in1=RL[rs][:, 0:n],
                  op=ALU.mult), [RL_b[rs]], [HID_b])
        for c in range(8):
            pi = pc % 6
            pc += 1
            for m in range(32):
                P.mm(k.ps[pi][:, 0:n], w2[:, m, c * 128:(c + 1) * 128], HID[:, m, 0:n], m == 0, m == 31,
                     [w2_b, HID_b], [k.psb[pi]])
            evac(k, Fo[:, c, 0:n], k.ps[pi][:, 0:n], c, [k.psb[pi]], [Fo_b])
        postnorm_res(k, layer, 1, Fo, Fo_b, Hs, Hsb, n, row, W)
        P.dma(h_ap(k, isctx, b, t0, n), Hs[:, :, 0:n], [Hsb], [h_buf(k, isctx, b)], key=Hsb, eng="pool")
    A.reset(m0)
    P.barrier()


def fm(v):
    v = np.asarray(v, np.float32)
    return np.ascontiguousarray(np.swapaxes(v.reshape(v.shape[:-1] + (8, 128)), -1, -2))


def core_inputs(inp, core):
    b0 = core * NB
    m = {}
    m["x"] = np.ascontiguousarray(inp["x"][b0:b0 + NB])
    m["ctx"] = np.ascontiguousarray(inp["ctx"][b0:b0 + NB])
    rows = np.stack([inp["c"][b0], inp["c"][b0 + 1], inp["c_ctx"]], 0)
    m["crow"] = np.ascontiguousarray(np.transpose(fm(rows), (1, 2, 0)))
    m["ada_w"] = inp["ada_w"]
    ab = np.asarray(inp["ada_b"], np.float32).reshape(4, 48, 128)
    m["ada_bT"] = np.ascontiguousarray(np.transpose(ab, (0, 2, 1)))
    m["gains"] = fm(inp["norm_gains"])
    m["mlp_w_in"] = inp["mlp_w_in"]
    m["mlp_w_out"] = inp["mlp_w_out"]
    m["ident"] = np.eye(128, dtype=np.float32)
    m["sc_w_in"] = inp["sc_w_in"][0]
    m["sc_convT"] = np.ascontiguousarray(np.transpose(fm(inp["sc_conv"][0]), (1, 2, 0)))
    m["sc_w_out"] = inp["sc_w_out"][0]
    m["hg_w_in"] = inp["hg_w_in"][0]
    m["hlbT"] = np.ascontiguousarray(np.transpose(fm(inp["hg_lower_bound"]), (1, 2, 0)))
    m["hg_normT"] = fm(inp["hg_norm"][0])
    m["hg_w_out"] = inp["hg_w_out"][0]
    idx = np.arange(128)
    same = (idx[:, None] // 32) == (idx[None, :] // 32)
    mf = (same & (idx[:, None] <= idx[None, :])).astype(np.float32)
    mb = (same & (idx[:, None] >= idx[None, :])).astype(np.float32)
    m["hg_masks"] = np.stack([mf, mb], 0)
    rm = np.ones((128, 256), np.float32)
    rm[:, 0::32] = 0.0
    m["hg_rmask"] = rm
    m["hg_rowm"] = (idx[:, None] // 32 == np.arange(4)[None, :]).astype(np.float32)

    def st(v):
        v = np.asarray(v, np.float32)
        lead = v.shape[:-3]
        x = v.shape[-1]
        v = v.reshape(lead + (32, 2, 64, x))
        nd = len(lead)
        v = np.transpose(v, tuple(range(nd)) + (nd + 1, nd + 2, nd, nd + 3))
        return np.ascontiguousarray(v.reshape(lead + (128, 32, x)))

    m["s5_LR"] = st(inp["s5_lam_re"][0][..., None])[..., 0]
    m["s5_LI"] = st(inp["s5_lam_im"][0][..., None])[..., 0]
    m["s5_DT"] = st(np.broadcast_to(inp["s5_log_dt"][0][:, :, None, None], (2, 64, 64, 1)))[..., 0]
    m["s5_BRE"] = st(inp["s5_b_re"][0])
    m["s5_BIM"] = st(inp["s5_b_im"][0])
    m["s5_CRE"] = st(np.transpose(inp["s5_c_re"][0], (0, 1, 3, 2)))
    m["s5_CIM"] = st(np.transpose(inp["s5_c_im"][0], (0, 1, 3, 2)))
    m["s5_dT"] = fm(inp["s5_d"][0])
    m["s5_w_glu"] = inp["s5_w_glu"][0]
    m["na_w_qkv"] = inp["na_w_qkv"][0]
    m["na_w_out"] = inp["na_w_out"][0]
    m["na_tbl"], m["na_valid"] = na_tables(inp["na_rpb"][0])
    return m


_NA_CACHE = {}


def na_tables(rpb):
    rpb = np.asarray(rpb, np.float32)
    kc = np.arange(64)[:, None]
    qc = np.arange(64)[None, :]
    dc = np.clip(kc - qc, -15, 15) + 15
    tbl = np.zeros((16, 31, 64, 64), np.float32)
    for e in range(31):
        dr = 22 - e
        if 0 <= dr <= 14:
            tbl[:, e] = rpb[:, dr][:, dc]
    if "valid" not in _NA_CACHE:
        qstart = np.clip(np.arange(64) - 8, 0, 48)
        colok = (kc >= qstart[None, :]) & (kc < qstart[None, :] + 16)
        valid = np.zeros((3, 8, 2, 64, 8, 64), np.float32)
        for v, (bd, r0) in enumerate(((0, 0), (1, 4), (7, 48))):
            for kt in range(8):
                for krl in range(2):
                    kr = r0 + 2 * kt + krl
                    for qrl in range(8):
                        qr = 8 * bd + qrl
                        st_ = min(max(qr - 4, 0), 56)
                        if st_ <= kr < st_ + 8:
                            valid[v, kt, krl, :, qrl, :] = colok
        _NA_CACHE["valid"] = valid.reshape(3, 8, 128, 512)
    return tbl, _NA_CACHE["valid"]


_NC_CACHE = {}


def run(inp, cfg, cores, trace=False):
    nc = build(cfg)
    in_maps = [core_inputs(inp, c) for c in cores]
    res = run_bass_kernel_spmd(nc, in_maps, core_ids=list(range(len(cores))), trace=trace)
    return res


def kernel(**inputs):
    inp = {k_: np.asarray(v) for k_, v in inputs.items()}
    res = run(inp, {}, list(range(NCORES)))
    out = np.concatenate([r["out"] for r in res.results], axis=0)
    return out.astype(np.float32)
```
